# Optimizing a Trainium2 kernel written in Bass

```python
import jax, jax.numpy as jnp
from jax import lax
import numpy as np

D_MODEL = 2048
BATCH = 4
SEQ = 2048
DEPTH = 1
DEC_BATCH = 128
DEC_SEQ = 1
PAST_LEN = 16384
PAGE_SIZE = 128

N_META = 16
H_RET = 8
DK_RET = D_MODEL // H_RET
DV_RET = D_MODEL // H_RET
DK_HG = 128
H_HG = D_MODEL // DK_HG
DV_HG = D_MODEL // H_HG
D_FF = ((8 * D_MODEL // 3 + 127) // 128) * 128
CHUNK_RET = 128
CHUNK_HG = 64
N_PROJ = 10
ALPHA = (2 * DEPTH) ** 0.25
BETA = (8 * DEPTH) ** -0.25
LN_EPS = 1e-5
ROPE_BASE = 10000.0

kernel_name = "hybrid_retention_hgrn2_macaron_step"


def layer_norm(x, g, b):
    xf = x.astype(jnp.float32)
    mu = jnp.mean(xf, axis=-1, keepdims=True)
    var = jnp.mean(jnp.square(xf - mu), axis=-1, keepdims=True)
    return ((xf - mu) * lax.rsqrt(var + LN_EPS)).astype(x.dtype) * g + b


def head_layer_norm(o):
    of = o.astype(jnp.float32)
    mu = jnp.mean(of, axis=-1, keepdims=True)
    var = jnp.mean(jnp.square(of - mu), axis=-1, keepdims=True)
    return ((of - mu) * lax.rsqrt(var + LN_EPS)).astype(o.dtype)


def head_rms_norm(o, g):
    of = o.astype(jnp.float32)
    ms = jnp.mean(jnp.square(of), axis=-1, keepdims=True)
    return (of * lax.rsqrt(ms + LN_EPS)).astype(o.dtype) * g


def swiglu(x, w_gate, w_up, w_down):
    return (jax.nn.silu(x @ w_gate) * (x @ w_up)) @ w_down


def rotary(x, pos):
    d = x.shape[-1]
    inv = ROPE_BASE ** (-jnp.arange(0, d, 2, dtype=jnp.float32) / d)
    ang = pos.astype(jnp.float32)[:, None] * inv[None, :]
    cos = jnp.cos(ang)[None, :, None, :].astype(x.dtype)
    sin = jnp.sin(ang)[None, :, None, :].astype(x.dtype)
    x1, x2 = x[..., 0::2], x[..., 1::2]
    return jnp.stack([x1 * cos - x2 * sin, x1 * sin + x2 * cos], axis=-1).reshape(x.shape)


def retention_log_decay():
    return jnp.log(1.0 - 2.0 ** (-5.0 - jnp.arange(H_RET, dtype=jnp.float32)))


def retention_chunk(S, q, k, v):
    C = q.shape[2]
    lg = retention_log_decay()
    idx = jnp.arange(C, dtype=jnp.float32)
    rel = idx[:, None] - idx[None, :]
    D = jnp.where(rel >= 0, jnp.exp(lg[:, None, None] * jnp.maximum(rel, 0.0)), 0.0).astype(q.dtype)
    inner = jnp.einsum('bhid,bhjd->bhij', q, k) * D
    q_dec = q * jnp.exp(lg[:, None] * (idx + 1.0))[:, :, None].astype(q.dtype)
    o = jnp.einsum('bhij,bhjv->bhiv', inner, v) + jnp.einsum('bhid,bhdv->bhiv', q_dec, S)
    k_dec = k * jnp.exp(lg[:, None] * (C - 1.0 - idx))[:, :, None].astype(k.dtype)
    S_new = jnp.exp(lg * C)[:, None, None].astype(S.dtype) * S + jnp.einsum('bhjd,bhjv->bhdv', k_dec, v)
    return S_new.astype(S.dtype), o


def hgrn2_chunk(S, q, k, v, logf):
    C = q.shape[2]
    b = jnp.cumsum(logf.astype(jnp.float32), axis=2)
    causal = jnp.tril(jnp.ones((C, C), dtype=bool))
    diff = b[:, :, :, None, :] - b[:, :, None, :, :]
    decay = jnp.exp(jnp.where(causal[None, None, :, :, None], diff, -jnp.inf)).astype(q.dtype)
    A = jnp.einsum('bhid,bhjd,bhijd->bhij', q, k, decay)
    o = jnp.einsum('bhij,bhjv->bhiv', A, v) + jnp.einsum('bhid,bhdv->bhiv', q * jnp.exp(b).astype(q.dtype), S)
    b_last = b[:, :, -1:, :]
    k_dec = k * jnp.exp(b_last - b).astype(k.dtype)
    S_new = jnp.exp(b_last[:, :, 0, :, None]) * S + jnp.einsum('bhjd,bhjv->bhdv', k_dec, v)
    return S_new.astype(S.dtype), o


def chunked_scan(chunk_fn, S0, seqs, chunk):
    S, o_meta = chunk_fn(S0, *[a[:, :, :N_META] for a in seqs])
    rest = [a[:, :, N_META:] for a in seqs]
    n = rest[0].shape[2] // chunk

    def to_chunks(a):
        B, H, L, d = a.shape
        return jnp.moveaxis(a.reshape(B, H, n, chunk, d), 2, 0)

    def step(carry, xs):
        return chunk_fn(carry, *xs)

    S, o_rest = lax.scan(step, S, tuple(to_chunks(a) for a in rest))
    B, H = o_rest.shape[1], o_rest.shape[2]
    o_rest = jnp.moveaxis(o_rest, 0, 2).reshape(B, H, n * chunk, o_rest.shape[-1])
    return S, jnp.concatenate([o_meta, o_rest], axis=2)


def token_mixer(x, pos, s_ret, s_hg, prompt, w_in, lb, hg_norm_g, w_out):
    B, L, _ = x.shape
    q_r, k_r, v_r, g_r, q_h, f_h, i_h, g_h, a_r, a_h = jnp.split(x @ w_in, N_PROJ, axis=-1)

    def to_heads(a, h):
        return jnp.swapaxes(a.reshape(B, L, h, -1), 1, 2)

    q_r = jnp.swapaxes(rotary(q_r.reshape(B, L, H_RET, DK_RET), pos), 1, 2)
    k_r = jnp.swapaxes(rotary(k_r.reshape(B, L, H_RET, DK_RET), pos), 1, 2) * (DK_RET ** -0.5)
    v_r = to_heads(v_r, H_RET)
    f = lb + (1.0 - lb) * jax.nn.sigmoid(f_h.astype(jnp.float32))
    logf = to_heads(jnp.log(f), H_HG)
    k_h = to_heads((1.0 - f).astype(x.dtype), H_HG)
    q_h = to_heads(jax.nn.silu(q_h), H_HG)
    i_h = to_heads(i_h, H_HG)

    if prompt:
        s_ret, o_r = chunked_scan(retention_chunk, s_ret, (q_r, k_r, v_r), CHUNK_RET)
        s_hg, o_h = chunked_scan(hgrn2_chunk, s_hg, (q_h, k_h, i_h, logf), CHUNK_HG)
    else:
        s_ret, o_r = retention_chunk(s_ret, q_r, k_r, v_r)
        s_hg, o_h = hgrn2_chunk(s_hg, q_h, k_h, i_h, logf)

    o_r = head_layer_norm(jnp.swapaxes(o_r, 1, 2)).reshape(B, L, D_MODEL) * jax.nn.silu(g_r)
    o_h = head_rms_norm(jnp.swapaxes(o_h, 1, 2), hg_norm_g.reshape(H_HG, DV_HG)).reshape(B, L, D_MODEL) * jax.nn.silu(g_h)
    y = jax.nn.sigmoid(a_r) * o_r + jax.nn.sigmoid(a_h) * o_h
    return y @ w_out, s_ret, s_hg


def decoder_layer(x, pos, s_ret, s_hg, prompt, ln1_g, ln1_b, f1_g, f1_u, f1_d, w_in, lb, hg_norm_g,
                  w_out, ln2_g, ln2_b, f2_g, f2_u, f2_d, ln3_g, ln3_b):
    x = layer_norm(ALPHA * x + 0.5 * swiglu(x, f1_g, f1_u, f1_d), ln1_g, ln1_b)
    m, s_ret, s_hg = token_mixer(x, pos, s_ret, s_hg, prompt, w_in, lb, hg_norm_g, w_out)
    x = layer_norm(ALPHA * x + m, ln2_g, ln2_b)
    x = layer_norm(ALPHA * x + 0.5 * swiglu(x, f2_g, f2_u, f2_d), ln3_g, ln3_b)
    return x, s_ret, s_hg


def setup_inputs(seed: int = 0) -> dict:
    key = jax.random.key(seed)
    ks = jax.random.split(key, 21)
    f32 = jnp.float32
    sD = D_MODEL ** -0.5
    sF = D_FF ** -0.5

    def nrm(k, shape, scale):
        return jax.random.normal(k, shape, f32) * scale

    return {
        "x_prompt": nrm(ks[0], (BATCH, SEQ, D_MODEL), 1.0),
        "x_sample": nrm(ks[1], (DEC_BATCH, DEC_SEQ, D_MODEL), 1.0),
        "state_ret": nrm(ks[2], (DEPTH, DEC_BATCH, H_RET, DK_RET, DV_RET), 0.5),
        "state_hgrn": nrm(ks[3], (DEPTH, DEC_BATCH, H_HG, DK_HG, DV_HG), 0.5),
        "meta_tokens": nrm(ks[4], (N_META, D_MODEL), 1.0),
        "ln1_g": 1.0 + nrm(ks[5], (DEPTH, D_MODEL), 0.02),
        "ln1_b": nrm(ks[6], (DEPTH, D_MODEL), 0.02),
        "ffn1_w_gate": nrm(ks[7], (DEPTH, D_MODEL, D_FF), sD),
        "ffn1_w_up": nrm(ks[8], (DEPTH, D_MODEL, D_FF), sD),
        "ffn1_w_down": nrm(ks[9], (DEPTH, D_FF, D_MODEL), sF * BETA),
        "w_in": nrm(ks[10], (DEPTH, D_MODEL, N_PROJ * D_MODEL), sD),
        "hgrn_lb_logits": nrm(ks[11], (DEPTH + 1, D_MODEL), 0.1),
        "hgrn_norm_g": 1.0 + nrm(ks[12], (DEPTH, D_MODEL), 0.02),
        "w_out": nrm(ks[13], (DEPTH, D_MODEL, D_MODEL), sD * BETA),
        "ln2_g": 1.0 + nrm(ks[14], (DEPTH, D_MODEL), 0.02),
        "ln2_b": nrm(ks[15], (DEPTH, D_MODEL), 0.02),
        "ffn2_w_gate": nrm(ks[16], (DEPTH, D_MODEL, D_FF), sD),
        "ffn2_w_up": nrm(ks[17], (DEPTH, D_MODEL, D_FF), sD),
        "ffn2_w_down": nrm(ks[18], (DEPTH, D_FF, D_MODEL), sF * BETA),
        "ln3_g": 1.0 + nrm(ks[19], (DEPTH, D_MODEL), 0.02),
        "ln3_b": nrm(ks[20], (DEPTH, D_MODEL), 0.02),
    }


def reference(x_prompt, x_sample, state_ret, state_hgrn, meta_tokens, ln1_g, ln1_b, ffn1_w_gate,
              ffn1_w_up, ffn1_w_down, w_in, hgrn_lb_logits, hgrn_norm_g, w_out, ln2_g, ln2_b,
              ffn2_w_gate, ffn2_w_up, ffn2_w_down, ln3_g, ln3_b):
    B = x_prompt.shape[0]
    dt = x_prompt.dtype
    lb_all = jnp.cumsum(jax.nn.softmax(hgrn_lb_logits.astype(jnp.float32), axis=0), axis=0)

    meta = jnp.broadcast_to(meta_tokens.astype(dt)[None], (B, N_META, D_MODEL))
    hp = jnp.concatenate([meta, x_prompt], axis=1)
    hs = x_sample
    pos_p = jnp.arange(N_META + SEQ, dtype=jnp.int32)
    pos_s = PAST_LEN + jnp.arange(DEC_SEQ, dtype=jnp.int32)

    rp_list, rs_list, gp_list, gs_list = [], [], [], []
    for l in range(DEPTH):
        lp = (ln1_g[l], ln1_b[l], ffn1_w_gate[l], ffn1_w_up[l], ffn1_w_down[l], w_in[l], lb_all[l],
              hgrn_norm_g[l], w_out[l], ln2_g[l], ln2_b[l], ffn2_w_gate[l], ffn2_w_up[l],
              ffn2_w_down[l], ln3_g[l], ln3_b[l])
        s_ret0 = jnp.zeros((B, H_RET, DK_RET, DV_RET), dt)
        s_hg0 = jnp.zeros((B, H_HG, DK_HG, DV_HG), dt)
        hp, rp, gp = decoder_layer(hp, pos_p, s_ret0, s_hg0, True, *lp)
        hs, rs, gs = decoder_layer(hs, pos_s, state_ret[l], state_hgrn[l], False, *lp)
        rp_list.append(rp)
        rs_list.append(rs)
        gp_list.append(gp)
        gs_list.append(gs)

    y_prompt = hp[:, N_META:]
    y_sample = hs
    state_ret_prompt = jnp.stack(rp_list, axis=0)
    state_ret_sample = jnp.stack(rs_list, axis=0)
    state_hgrn_prompt = jnp.stack(gp_list, axis=0)
    state_hgrn_sample = jnp.stack(gs_list, axis=0)
    return (y_prompt, y_sample, state_ret_prompt, state_ret_sample, state_hgrn_prompt, state_hgrn_sample)
```

```python
import contextlib
import numpy as np
import concourse.bass as bass
import concourse.mybir as mybir
from concourse.bass_utils import run_bass_kernel_spmd

F32 = mybir.dt.float32
F32R = mybir.dt.float32r
AF = mybir.ActivationFunctionType
ALU = mybir.AluOpType

D = 2048
KC = 16
FF = 5504
FFC = 43
NMETA = 16
HR = 8
HH = 16
PAST = 16384
ALPHA = 2.0 ** 0.25
EPS = 1e-5
NPIECE = 434
GROUPS = [8, 8, 8, 8, 8, 2, 1]
NSMP = 16


class Ev:
    __slots__ = ("sem", "val", "eng")

    def __init__(self, sem, val, eng):
        self.sem, self.val, self.eng = sem, val, eng


class Buf:
    def __init__(self, name, t=None):
        self.name = name
        self.t = t
        self.w = None
        self.r = {}
        self.dsem = None
        self.dcnt = 0


class Eng:
    def __init__(self, name, sem):
        self.name = name
        self.sem = sem
        self.cnt = 0
        self.prog = []
        self.waited = {}


class Sched:
    def __init__(self, nc, stack):
        self.nc = nc
        self.stack = stack
        self.engs = {}
        for n in ("pe", "act", "dve", "pool", "sp"):
            self.engs[n] = Eng(n, stack.enter_context(nc.semaphore("sem_" + n)))
        self.dsems = []
        self.final = []

    def _deps(self, E, R, W):
        deps = []
        for b in R:
            if b.w is not None:
                deps.append(b.w)
        for b in W:
            if b.w is not None:
                deps.append(b.w)
            deps.extend(b.r.values())
        for ev in deps:
            if ev.eng is E and E.name == "pe":
                continue
            k = id(ev.sem)
            if E.waited.get(k, 0) < ev.val:
                E.prog.append(("w", ev.sem, ev.val))
                E.waited[k] = ev.val

    @staticmethod
    def _mark(ev, R, W):
        for b in R:
            k = id(ev.sem)
            o = b.r.get(k)
            if o is None or o.val < ev.val:
                b.r[k] = ev
        for b in W:
            b.w = ev
            b.r = {}

    def I(self, en, _opn, R=(), W=(), inc=True, **kw):
        fn = (_opn, kw)
        E = self.engs[en]
        self._deps(E, R, W)
        if inc:
            E.cnt += 1
            ev = Ev(E.sem, E.cnt, E)
        else:
            ev = Ev(E.sem, E.cnt + 1, E)
        E.prog.append(("i", fn, inc))
        self._mark(ev, R, W)
        return ev

    def DMA(self, out, in_, sb, R=(), W=(), final=False):
        E = self.engs["sp"]
        self._deps(E, R, W)
        if sb.dsem is None:
            sb.dsem = self.stack.enter_context(self.nc.semaphore("dsem_%d" % len(self.dsems)))
            self.dsems.append(sb.dsem)
        sb.dcnt += 16
        ev = Ev(sb.dsem, sb.dcnt, None)
        E.prog.append(("d", out, in_, sb.dsem))
        self._mark(ev, R, W)
        if final:
            self.final.append(ev)
        return ev

    def retire(self, old, new):
        evs = []
        for b in old:
            if b.w is not None:
                evs.append(b.w)
            evs.extend(b.r.values())
        for b in new:
            for ev in evs:
                k = id(ev.sem)
                o = b.r.get(k)
                if o is None or o.val < ev.val:
                    b.r[k] = ev

    def finish(self):
        E = self.engs["sp"]
        for ev in self.final:
            k = id(ev.sem)
            if E.waited.get(k, 0) < ev.val:
                E.prog.append(("w", ev.sem, ev.val))
                E.waited[k] = ev.val

    def replay(self, en, h):
        E = self.engs[en]
        for it in E.prog:
            if it[0] == "w":
                h.wait_ge(it[1], it[2])
            elif it[0] == "i":
                ins = getattr(h, it[1][0])(**it[1][1])
                if it[2]:
                    ins.then_inc(E.sem, 1)
            else:
                h.dma_start(out=it[1], in_=it[2]).then_inc(it[3], 16)


PRE = 8


def tile_plan(nmain):
    nt = nmain // 4
    out = []
    col = 0
    for part in range(2):
        tiles = []
        for t in range(nt):
            pre = t == 0
            smp = (part == 1) and (t == nt - 1)
            n = 512 + (PRE if pre else 0) + (16 if smp else 0)
            chunks = []
            o = 0
            if pre:
                chunks.append((0, PRE))
                o = PRE
            for i in range(4):
                chunks.append((o + 128 * i, 128))
            soff = o + 512
            g0 = 0 if t == 0 else PRE + 512 * t
            tiles.append(dict(c0=col, n=n, pre=pre, smp=smp, chunks=chunks, soff=soff, g0=g0))
            col += n
        out.append(tiles)
    return out[0], out[1], col


C_ID = 0
C_CAUS = 128
C_DM = 256
C_KD128 = C_DM + 8 * 128
C_KD16 = C_KD128 + 8
C_RD = C_KD16 + 8
C_SEL = C_RD + 8
C_OH = C_SEL + 256
C_OHK = C_OH + 16
C_PDEC = C_OHK + 16
C_END = C_PDEC + 128
V_LN = 0
V_LB0 = 96
V_LB1 = 112
V_HG = 128
V_END = 144


def build(nmain, stop_stage=99):
    ptiles, otiles, NTOT = tile_plan(nmain)
    tiles = ptiles + otiles
    NT = len(tiles)
    NOWN = NTOT - otiles[0]["c0"]
    NMAX = max(t["n"] for t in tiles)
    N = NMAX
    nc = bass.Bass("TRN2", target_bir_lowering=False)
    dr = lambda name, shape, kind: nc.dram_tensor(name, shape, F32, kind=kind).ap()
    xin = dr("xin", [NTOT, D], "ExternalInput")
    wall = dr("wall", [NPIECE, 128, 2048], "ExternalInput")
    rot = dr("rot", [NT, 128, 2 * NMAX], "ExternalInput")
    vecs = dr("vecs", [128, V_END], "ExternalInput")
    cst = dr("cst", [128, C_END], "ExternalInput")
    sret_in = dr("sret_in", [NSMP * HR * 128, 512], "ExternalInput")
    shg_in = dr("shg_in", [NSMP * HH * 128, 128], "ExternalInput")
    yout = dr("y", [NOWN, D], "ExternalOutput")
    selv = dr("selv", [128, 1], "ExternalInput")
    sret_p = dr("sret_p", [HR * 128, 512], "ExternalOutput")
    shg_p = dr("shg_p", [HH * 128, 128], "ExternalOutput")
    sret_s = dr("sret_s", [NSMP * HR * 128, 512], "ExternalOutput")
    shg_s = dr("shg_s", [NSMP * HH * 128, 128], "ExternalOutput")

    with contextlib.ExitStack() as stack:
        S = Sched(nc, stack)
        I, DMA = S.I, S.DMA

        def sb(name, shape):
            return stack.enter_context(nc.sbuf_tensor(name, shape, F32))

        XA_t = sb("XA", [128, KC, NMAX])
        YA_t = sb("YA", [128, KC * NMAX])
        HG_t = sb("HG", [128, KC, NMAX])
        ST_t = [sb("ST%d" % i, [128, 2048]) for i in range(3)]
        WC_t = [sb("WC%d" % i, [128, 2048]) for i in range(2)]
        SR_t = sb("SR", [128, HR, 512])
        SH_t = sb("SH", [128, HH, 128])
        CT_t = sb("CT", [128, 2176])
        CR_t = sb("CR", [128, 4 * NMAX + 1280])
        ON1_t = sb("ON1", [128, 128])
        CS_t = sb("CS", [128, C_END])
        RT_t = sb("RT", [128, 2, NMAX])
        VC_t = sb("VC", [128, V_END])
        SM_t = sb("SM", [128, 512])
        SS_t = [sb("SS%d" % i, [128, 512]) for i in range(2)]
        QM_t = sb("QM", [128, 2, 256])
        PS = [Buf("ps%d" % i, stack.enter_context(nc.psum_tensor("ps%d" % i, [128, 512], F32))) for i in range(8)]
        PO = PS[7]
        PO2 = PS[6]
        ps_ctr = [0]

        pend = [None]

        def nps():
            p = PS[ps_ctr[0] % 6]
            ps_ctr[0] += 1
            if pend[0] is not None and pend[0][0] is p:
                f = pend[0][1]
                pend[0] = None
                f()
            return p

        def r(ap):
            return ap.bitcast(F32R)

        XA = Buf("XA", XA_t)
        YA = Buf("YA", YA_t)
        HG = [Buf("HG0"), Buf("HG1")]
        ST = [Buf("ST%d" % i, ST_t[i]) for i in range(3)]
        WC = [(Buf("WC%da" % i, WC_t[i]), Buf("WC%db" % i, WC_t[i])) for i in range(2)]
        SSb = [Buf("SS%d" % i, SS_t[i]) for i in range(2)]
        SSh = [Buf("SSh%d" % k, SS_t[k // 4][:, (k % 4) * 128:(k % 4 + 1) * 128]) for k in range(8)]
        SRb = [Buf("SR%d" % g) for g in range(HR)]
        SHb = [Buf("SH%d" % g) for g in range(HH)]
        CSb = Buf("CS", CS_t)
        RTb = Buf("RT", RT_t)
        VCb = Buf("VC", VC_t)
        SMb = Buf("SM", SM_t)
        CTb = Buf("CT", CT_t)
        QMb = Buf("QM")
        YAv = YA_t[:].rearrange("p (a b) -> p a b", a=KC)

        ident = CS_t[:, C_ID:C_ID + 128]
        caus = CS_t[:, C_CAUS:C_CAUS + 128]
        sel3 = CS_t[:, C_SEL:C_SEL + 256].rearrange("p (a b) -> p a b", a=16)
        onesD = ON1_t[:, :]
        ones1 = SM_t[:, 128:256]
        LBc = SM_t[:, 256:272]
        OMLc = SM_t[:, 272:288]
        zer = SM_t[:, 320:448]

        def epsc(C=128):
            return SM_t[:C, 288:289]

        def ct(a, b, C=128):
            return CT_t[:C, a:b]

        def cr(a, b, C=128):
            return CR_t[:C, a:b]

        DMA(CS_t[:], cst[:, :], CSb, W=[CSb])
        DMA(VC_t[:], vecs[:, :], VCb, W=[VCb])
        SEL_t = sb("SELV", [128, 1])
        SELb = Buf("SELV", SEL_t)
        DMA(SEL_t[:], selv[:, :], SELb, W=[SELb])
        I("pool", "memset", W=[SMb], ap=SM_t[:, 128:256], constant=1.0)
        I("pool", "memset", W=[SMb], ap=SM_t[:, 288:289], constant=EPS)
        I("pool", "memset", W=[SMb], ap=SM_t[:, 320:448], constant=0.0)
        I("dve", "tensor_scalar", R=[SMb], W=[SMb], out=r(onesD), in0=ones1, scalar1=1.0 / D, scalar2=None, op0=ALU.mult)
        I("dve", "tensor_tensor", R=[VCb], W=[SMb], out=SM_t[:, 296:312], in0=VC_t[:, V_LB0:V_LB0 + 16],
          in1=VC_t[:, V_LB1:V_LB1 + 16], op=ALU.subtract)
        I("act", "activation", R=[SMb], W=[SMb], out=LBc, in_=SM_t[:, 296:312], func=AF.Sigmoid)
        I("dve", "tensor_scalar", R=[SMb], W=[SMb], out=OMLc, in0=LBc, scalar1=-1.0, scalar2=1.0, op0=ALU.mult, op1=ALU.add)
        for g in range(HR):
            for q4 in range(4):
                I("pool", "tensor_copy", R=[SMb], W=[SRb[g]], out=r(SR_t[:, g, q4 * 128:(q4 + 1) * 128]), in_=zer)
        for g in range(HH):
            I("pool", "tensor_copy", R=[SMb], W=[SHb[g]], out=r(SH_t[:, g, :]), in_=zer)

        pc = [0]

        seqc = [0]
        wcc = [0]
        conv_split = [1024]

        pref = {}

        def next_piece(nxt=-1):
            p = pc[0]
            pc[0] += 1
            wc = pref.pop(p, None)
            pref.clear()
            if wc is None:
                wc = issue_piece(p)
            n2 = p + 1 if nxt == -1 else nxt
            if n2 is not None and n2 < NPIECE:
                pref[n2] = issue_piece(n2)
            return wc

        def issue_piece(p):
            q = seqc[0]
            seqc[0] += 1
            st = ST[q % 3]
            wc = WC[wcc[0] % 2]
            wcc[0] += 1
            DMA(st.t[:], wall[p % NPIECE], st, W=[st])
            sp_ = conv_split[0]
            I("act", "activation", R=[st], W=[wc[0]], out=r(wc[0].t[:, :sp_]), in_=st.t[:, :sp_], func=AF.Copy)
            I("dve", "tensor_copy", R=[st], W=[wc[1]], out=r(wc[1].t[:, sp_:]), in_=st.t[:, sp_:])
            return wc

        def halves_of(n):
            return [(0, n // 2), (n // 2, n // 2)]

        def gemm_fm(wc, n, evac, xbufs=None, xt=None, bg=None):
            xbufs = [XA] if xbufs is None else xbufs
            xt = XA_t if xt is None else xt
            wv = wc[0].t[:].rearrange("p (a b) -> p a b", a=KC)
            for hf, (c0, nh) in enumerate(halves_of(n)):
                ps = nps()
                for kc in range(KC):
                    I("pe", "matmul", R=[wc[0], wc[1]] + xbufs, W=[ps], inc=(kc == KC - 1),
                      out=ps.t[:, :nh], lhsT=r(wv[:, kc, :]), rhs=r(xt[:, kc, c0:c0 + nh]),
                      start=(kc == 0), stop=(kc == KC - 1))
                if bg is not None:
                    pend[0] = (ps, lambda ps=ps, hf=hf, c0=c0, nh=nh: evac(ps, hf, c0, nh))
                    bg()
                    if pend[0] is not None:
                        f = pend[0][1]
                        pend[0] = None
                        f()
                else:
                    evac(ps, hf, c0, nh)

        def layer_norm(n, lnidx):
            gcol = V_LN + lnidx * 32
            bcol = gcol + 16
            MEAN, MSQ, VAR, LNV, RSTD, NMR = (CT_t[:, k * 272:(k + 1) * 272] for k in range(6))
            for hf, (c0, nh) in enumerate(halves_of(n)):
                I("act", "activation", R=[YA], W=[HG[0], HG[1]], out=r(HG_t[:, :, c0:c0 + nh]), in_=YAv[:, :, c0:c0 + nh],
                  func=AF.Square)
                p1 = nps()
                for kc in range(KC):
                    I("pe", "matmul", R=[SMb, YA], W=[p1], inc=(kc == KC - 1), out=p1.t[:, :nh], lhsT=r(onesD),
                      rhs=r(YAv[:, kc, c0:c0 + nh]), start=(kc == 0), stop=(kc == KC - 1))
                p2 = nps()
                for kc in range(KC):
                    I("pe", "matmul", R=[SMb, HG[0], HG[1]], W=[p2], inc=(kc == KC - 1), out=p2.t[:, :nh], lhsT=r(onesD),
                      rhs=r(HG_t[:, kc, c0:c0 + nh]), start=(kc == 0), stop=(kc == KC - 1))
                I("act", "activation", R=[p1], W=[CTb], out=MEAN[:, :nh], in_=p1.t[:, :nh], func=AF.Copy)
                I("dve", "tensor_tensor", R=[CTb], W=[CTb], out=MSQ[:, :nh], in0=MEAN[:, :nh], in1=MEAN[:, :nh], op=ALU.mult)
                I("dve", "tensor_tensor", R=[p2, CTb], W=[CTb], out=VAR[:, :nh], in0=p2.t[:, :nh], in1=MSQ[:, :nh], op=ALU.subtract)
                I("act", "activation", R=[CTb, SMb], W=[CTb], out=LNV[:, :nh], in_=VAR[:, :nh], func=AF.Ln, bias=epsc(), scale=1.0)
                I("act", "activation", R=[CTb], W=[CTb], out=RSTD[:, :nh], in_=LNV[:, :nh], func=AF.Exp, scale=-0.5)
                I("dve", "scalar_tensor_tensor", R=[CTb], W=[CTb], out=NMR[:, :nh], in0=MEAN[:, :nh], scalar=-1.0,
                  in1=RSTD[:, :nh], op0=ALU.mult, op1=ALU.mult)
                I("dve", "tensor_tensor", R=[YA, CTb], W=[XA], out=r(XA_t[:, :, c0:c0 + nh]), in0=YAv[:, :, c0:c0 + nh],
                  in1=RSTD[:, :nh].unsqueeze(1).broadcast_to([128, KC, nh]), op=ALU.mult)
                I("dve", "tensor_tensor", R=[XA, CTb], W=[XA], out=r(XA_t[:, :, c0:c0 + nh]), in0=XA_t[:, :, c0:c0 + nh],
                  in1=NMR[:, :nh].unsqueeze(1).broadcast_to([128, KC, nh]), op=ALU.add)
                for kc in range(KC):
                    I("dve", "tensor_scalar", R=[XA, VCb], W=[XA], out=r(XA_t[:, kc, c0:c0 + nh]), in0=XA_t[:, kc, c0:c0 + nh],
                      scalar1=VC_t[:, gcol + kc:gcol + kc + 1], scalar2=VC_t[:, bcol + kc:bcol + kc + 1],
                      op0=ALU.mult, op1=ALU.add)

        def ffn_ln(n, lnidx, after=-1):
            SG = [CT_t[:, 1632:1632 + 272], CT_t[:, 1904:1904 + 272]]
            SGb = [Buf("SG0"), Buf("SG1")]
            S.retire([CTb], SGb)
            I("dve", "tensor_scalar", R=[XA], W=[YA], out=r(YAv[:, :, :n]), in0=XA_t[:, :, :n], scalar1=ALPHA, scalar2=None,
              op0=ALU.mult)
            for gi, g in enumerate(GROUPS):
                hg = HG[gi % 2]
                hgv = HG_t[:, (gi % 2) * 8:(gi % 2) * 8 + 8, :]
                for jj in range(g):
                    def ev_gate(ps, hf, c0, nh):
                        I("act", "activation", R=[ps], W=[SGb[hf]], out=SG[hf][:, :nh], in_=ps.t[:, :nh], func=AF.Silu)
                    gemm_fm(next_piece(), n, ev_gate)

                    def ev_up(ps, hf, c0, nh, jj=jj, hg=hg, hgv=hgv):
                        I("dve", "tensor_tensor", R=[ps, SGb[hf]], W=[hg], out=r(hgv[:, jj, c0:c0 + nh]), in0=ps.t[:, :nh],
                          in1=SG[hf][:, :nh], op=ALU.mult)
                    gemm_fm(next_piece(), n, ev_up)
                ncols = 2048 // g
                for dp in range(g):
                    wd = next_piece(nxt=after) if (gi == len(GROUPS) - 1 and dp == g - 1) else next_piece()
                    wv = wd[0].t[:].rearrange("p (a b) -> p a b", a=g)
                    for mt in range(ncols // 128):
                        m = dp * (ncols // 128) + mt
                        for hf, (c0, nh) in enumerate(halves_of(n)):
                            ps = nps()
                            for kc in range(g):
                                I("pe", "matmul", R=[wd[0], wd[1], hg], W=[ps], inc=(kc == g - 1), out=ps.t[:, :nh],
                                  lhsT=r(wv[:, kc, mt * 128:(mt + 1) * 128]), rhs=r(hgv[:, kc, c0:c0 + nh]),
                                  start=(kc == 0), stop=(kc == g - 1))
                            I("dve", "scalar_tensor_tensor", R=[ps, YA], W=[YA], out=r(YAv[:, m, c0:c0 + nh]), in0=ps.t[:, :nh],
                              scalar=0.5, in1=YAv[:, m, c0:c0 + nh], op0=ALU.mult, op1=ALU.add)
            S.retire(SGb, [CTb])
            layer_norm(n, lnidx)

        def blocks_of(tl):
            bl = list(tl["chunks"])
            if tl["smp"]:
                bl.append((tl["soff"], 16))
            return bl

        io_ctr = [0]

        def load_x(tl):
            for (off, C) in blocks_of(tl):
                io = ST[seqc[0] % 3]
                seqc[0] += 1
                DMA(io.t[:C, :], xin[tl["c0"] + off:tl["c0"] + off + C, :], io, W=[io])
                for k4 in range(4):
                    ps = nps()
                    for q in range(4):
                        kc = k4 * 4 + q
                        I("pe", "transpose", R=[io, CSb], W=[ps], inc=(q == 3), out=ps.t[:, q * 128:q * 128 + C],
                          in_=io.t[:C, kc * 128:(kc + 1) * 128], identity=ident[:C, :C])
                    I("act", "activation", R=[ps], W=[XA], out=r(XA_t[:, k4 * 4:k4 * 4 + 4, off:off + C]),
                      in_=ps.t[:].rearrange("p (a b) -> p a b", a=4)[:, :, :C], func=AF.Copy)

        def store_y(tl):
            for (off, C) in blocks_of(tl):
                io = ST[seqc[0] % 3]
                seqc[0] += 1
                for k4 in range(4):
                    ps = nps()
                    for q in range(4):
                        kc = k4 * 4 + q
                        I("pe", "transpose", R=[XA, CSb], W=[ps], inc=(q == 3), out=ps.t[:C, q * 128:(q + 1) * 128],
                          in_=XA_t[:, kc, off:off + C], identity=ident)
                    I("dve", "tensor_copy", R=[ps], W=[io], out=io.t[:C, k4 * 512:(k4 + 1) * 512], in_=ps.t[:C, :])
                DMA(yout[tl["c0"] - otiles[0]["c0"] + off:tl["c0"] - otiles[0]["c0"] + off + C, :], io.t[:C, :], io, R=[io], final=True)

        YS = [Buf("YS%d" % k) for k in range(16)]

        def ysl(k, w=1):
            return YA_t[:, k * N:(k + w) * N]

        def y3(k):
            return YA_t[:, k * N:(k + 2) * N].rearrange("p (a b) -> p a b", a=2)
        RAW, QR, KR, VF, GR = y3(0), y3(2), y3(4), y3(8), y3(10)
        bRAW, bQR, bKR, bVF, bGR = [YS[0], YS[1]], [YS[2], YS[3]], [YS[4], YS[5]], [YS[8], YS[9]], [YS[10], YS[11]]
        T1, T2, TA = ysl(6), ysl(7), ysl(12)
        bT1, bT2, bTA = [YS[6]], [YS[7]], [YS[12]]
        HSET = [dict(QS=0, FS=1, LOGF=6, KH=7, TB=12, IF=13, GH=14),
                dict(QS=4, FS=5, LOGF=8, KH=9, IF=10, GH=11, TB=15)]
        YT = HG_t
        COS, SIN = RT_t[:, 0, :], RT_t[:, 1, :]
        oW = [0, N, 2 * N, 3 * N]
        oVT = [4 * N, 4 * N + 256]
        oKD = [4 * N + 512, 4 * N + 768]
        oPT = [4 * N + 1024, 4 * N + 1152]
        oBW = 0
        oO = [N, N + 256]
        oON = [N + 512, N + 768]
        oTBS = [N + 1024, N + 1280]
        oSTAT = [N + 1536, N + 1552]
        oAUX = N + 1568
        bW = [Buf("cW%d" % k) for k in range(4)]
        bVT, bKD, bPT = ([Buf(nm + str(k)) for k in range(2)] for nm in ("cVT", "cKD", "cPT"))
        bO, bON, bTBS, bSTT = ([Buf(nm + str(k)) for k in range(2)] for nm in ("cO", "cON", "cTBS", "cSTT"))
        bBW, bAUX = Buf("cBW"), Buf("cAUX")
        CXALL = bW + bVT + bKD + bPT + bO + bON + bTBS + bSTT + [bBW, bAUX]
        XSB = [bW[0], bW[1]]
        M = dict(YTb=None)

        def rstd_from(var_ap, C, par):
            o = oSTAT[par]
            I("act", "activation", R=[bSTT[par], SMb], W=[bSTT[par]], out=ct(o + 8, o + 9, C), in_=var_ap, func=AF.Ln, bias=epsc(C), scale=1.0)
            I("act", "activation", R=[bSTT[par]], W=[bSTT[par]], out=ct(o + 9, o + 10, C), in_=ct(o + 8, o + 9, C), func=AF.Exp, scale=-0.5)

        def ret_norm(C, par):
            o = oSTAT[par]
            I("dve", "bn_stats", R=[bO[par]], W=[bSTT[par]], out=ct(o, o + 6, C), in_=ct(oO[par], oO[par] + 256, C))
            I("dve", "bn_aggr", R=[bSTT[par]], W=[bSTT[par]], out=ct(o + 6, o + 8, C), in_=ct(o, o + 6, C))
            rstd_from(ct(o + 7, o + 8, C), C, par)
            I("dve", "tensor_scalar", R=[bO[par], bSTT[par]], W=[bON[par]], out=ct(oON[par], oON[par] + 256, C),
              in0=ct(oO[par], oO[par] + 256, C), scalar1=ct(o + 6, o + 7, C), scalar2=ct(o + 9, o + 10, C),
              op0=ALU.subtract, op1=ALU.mult)

        def ret_gate(g, off, C, par):
            YTb = M["YTb"]
            for e in range(2):
                pt = nps()
                I("pe", "transpose", R=[bON[par], CSb], W=[pt], out=pt.t[:, :C],
                  in_=ct(oON[par] + e * 128, oON[par] + (e + 1) * 128, C), identity=ident[:C, :C])
                I("dve", "tensor_tensor", R=[pt, bGR[e]], W=[YTb[2 * g + e]], out=r(YT[:, 2 * g + e, off:off + C]), in0=pt.t[:, :C],
                  in1=GR[:, e, off:off + C], op=ALU.mult)

        def hg_norm(C, pa, par):
            o = oSTAT[par]
            I("act", "activation", R=[pa], W=[bTBS[par], bSTT[par]], out=ct(oTBS[par], oTBS[par] + 128, C), in_=pa.t[:C, :128],
              func=AF.Square, accum_out=ct(o + 10, o + 11, C))
            I("dve", "tensor_scalar", R=[bSTT[par]], W=[bSTT[par]], out=ct(o + 11, o + 12, C), in0=ct(o + 10, o + 11, C),
              scalar1=1.0 / 128, scalar2=None, op0=ALU.mult)
            rstd_from(ct(o + 11, o + 12, C), C, par)
            I("dve", "tensor_scalar", R=[pa, bSTT[par]], W=[bON[par]], out=ct(oON[par], oON[par] + 128, C), in0=pa.t[:C, :128],
              scalar1=ct(o + 9, o + 10, C), scalar2=None, op0=ALU.mult)

        def hg_gate(hh, hs, off, C, par):
            YTb = M["YTb"]
            GH = ysl(hs["GH"])
            pt = nps()
            I("pe", "transpose", R=[bON[par], CSb], W=[pt], out=pt.t[:, :C], in_=ct(oON[par], oON[par] + 128, C), identity=ident[:C, :C])
            I("dve", "tensor_tensor", R=[pt, YS[hs["GH"]]], W=[bTBS[par]], out=CT_t[:, oTBS[par]:oTBS[par] + C], in0=pt.t[:, :C],
              in1=GH[:, off:off + C], op=ALU.mult)
            I("dve", "tensor_tensor", R=[bTBS[par], YTb[hh]], W=[YTb[hh]], out=r(YT[:, hh, off:off + C]),
              in0=CT_t[:, oTBS[par]:oTBS[par] + C], in1=YT[:, hh, off:off + C], op=ALU.add)

        def rotary(dst, bdst, n):
            c_, s_ = COS, SIN
            I("dve", "tensor_tensor", R=[bRAW[0], RTb], W=bT1, out=r(T1[:, :n]), in0=RAW[:, 0, :n], in1=c_[:, :n], op=ALU.mult)
            I("pool", "tensor_tensor", R=[bRAW[1], RTb], W=bT2, out=r(T2[:, :n]), in0=RAW[:, 1, :n], in1=s_[:, :n], op=ALU.mult)
            I("dve", "tensor_tensor", R=bT1 + bT2, W=[bdst[0]], out=r(dst[:, 0, :n]), in0=T1[:, :n], in1=T2[:, :n], op=ALU.subtract)
            I("dve", "tensor_tensor", R=[bRAW[0], RTb], W=bT1, out=r(T1[:, :n]), in0=RAW[:, 0, :n], in1=s_[:, :n], op=ALU.mult)
            I("pool", "tensor_tensor", R=[bRAW[1], RTb], W=bT2, out=r(T2[:, :n]), in0=RAW[:, 1, :n], in1=c_[:, :n], op=ALU.mult)
            I("dve", "tensor_tensor", R=bT1 + bT2, W=[bdst[1]], out=r(dst[:, 1, :n]), in0=T1[:, :n], in1=T2[:, :n], op=ALU.add)

        def ret_chain(g, tl, gam):
            srg = SR_t[:, g, :].rearrange("p (a b) -> p a b", a=2)
            chunks = tl["chunks"]
            nch = len(chunks)

            def stB(c):
                off, C = chunks[c]
                par = c % 2
                kdcol = (C_KD128 if C == 128 else C_KD16) + g
                VT, KD, PT = cr(oVT[par], oVT[par] + 256, C), cr(oKD[par], oKD[par] + 256, C), cr(oPT[par], oPT[par] + C, C)
                pv = nps()
                for e in range(2):
                    I("pe", "transpose", R=[bVF[e], CSb], W=[pv], inc=(e == 1), out=pv.t[:C, e * 128:(e + 1) * 128],
                      in_=VF[:, e, off:off + C], identity=ident)
                I("act", "activation", R=[pv], W=[bVT[par]], out=r(VT), in_=pv.t[:C, :256], func=AF.Copy)
                pk = nps()
                for e in range(2):
                    I("pe", "transpose", R=[bKR[e], CSb], W=[pk], inc=(e == 1), out=pk.t[:C, e * 128:(e + 1) * 128],
                      in_=KR[:, e, off:off + C], identity=ident)
                I("dve", "tensor_scalar", R=[pk, CSb], W=[bKD[par]], out=r(KD), in0=pk.t[:C, :256], scalar1=CS_t[:C, kdcol:kdcol + 1],
                  scalar2=None, op0=ALU.mult)
                p3 = nps()
                for e in range(2):
                    I("pe", "matmul", R=[bKR[e], bQR[e]], W=[p3], inc=(e == 1), out=p3.t[:C, :C], lhsT=r(KR[:, e, off:off + C]),
                      rhs=r(QR[:, e, off:off + C]), start=(e == 0), stop=(e == 1))
                I("dve", "tensor_tensor", R=[p3, CSb], W=[bPT[par]], out=r(PT), in0=p3.t[:C, :C],
                  in1=CS_t[:C, C_DM + g * 128:C_DM + g * 128 + C], op=ALU.mult)

            def stC(c):
                off, C = chunks[c]
                par = c % 2
                VT, PT = cr(oVT[par], oVT[par] + 256, C), cr(oPT[par], oPT[par] + C, C)
                pb = nps()
                for e in range(2):
                    I("pe", "matmul", R=[bQR[e], SRb[g]], W=[pb], inc=(e == 1), out=pb.t[:C, :256], lhsT=r(QR[:, e, off:off + C]),
                      rhs=r(srg[:, e, :]), start=(e == 0), stop=(e == 1))
                I("dve", "tensor_scalar", R=[pb, CSb], W=[bTBS[par]], out=ct(oTBS[par], oTBS[par] + 256, C), in0=pb.t[:C, :256],
                  scalar1=CS_t[:C, C_RD + g:C_RD + g + 1], scalar2=None, op0=ALU.mult)
                pa = nps()
                I("pe", "matmul", R=[bPT[par], bVT[par]], W=[pa], out=pa.t[:C, :256], lhsT=r(PT), rhs=r(VT), start=True, stop=True)
                I("dve", "tensor_tensor", R=[pa, bTBS[par]], W=[bO[par]], out=ct(oO[par], oO[par] + 256, C), in0=pa.t[:C, :256],
                  in1=ct(oTBS[par], oTBS[par] + 256, C), op=ALU.add)
                for e in range(2):
                    pS = nps()
                    I("pe", "matmul", R=[bKD[par], bVT[par]], W=[pS], out=pS.t[:, :256],
                      lhsT=r(CR_t[:C, oKD[par] + e * 128:oKD[par] + (e + 1) * 128]), rhs=r(VT), start=True, stop=True)
                    I("dve", "scalar_tensor_tensor", R=[pS, SRb[g]], W=[SRb[g]], out=r(srg[:, e, :]), in0=srg[:, e, :],
                      scalar=float(gam ** C), in1=pS.t[:, :256], op0=ALU.mult, op1=ALU.add)
                ret_norm(C, par)

            def stD(c):
                off, C = chunks[c]
                ret_gate(g, off, C, c % 2)
            for k in range(nch + 2):
                if k < nch:
                    stB(k)
                if 0 <= k - 1 < nch:
                    stC(k - 1)
                if 0 <= k - 2 < nch:
                    stD(k - 2)
                yield

        def hg_chain(hh, hs, tl):
            sh = SH_t[:, hh, :]
            QS, KH, LOGF, IF = ysl(hs["QS"]), ysl(hs["KH"]), ysl(hs["LOGF"]), ysl(hs["IF"])
            yQS, yKH, yLOGF, yIF = YS[hs["QS"]], YS[hs["KH"]], YS[hs["LOGF"]], YS[hs["IF"]]
            chunks = tl["chunks"]
            nch = len(chunks)
            ncol = chunks[-1][0] + chunks[-1][1]
            W0, W1, W2, W3 = (CR_t[:, o:o + N] for o in oW)
            Bw = CT_t[:, oBW:oBW + N]
            for (off, C) in chunks:
                I("dve", "tensor_tensor_scan", R=[yLOGF, SMb], W=[bBW], out=Bw[:, off:off + C], data0=ones1[:, :C],
                  data1=LOGF[:, off:off + C], initial=0.0, op0=ALU.mult, op1=ALU.add)
            for c, (off, C) in enumerate(chunks):
                mid = off + max(C // 2 - 1, 0)
                I("dve", "tensor_scalar", R=[bBW], W=[bAUX], out=CT_t[:, oAUX + c:oAUX + c + 1], in0=Bw[:, mid:mid + 1], scalar1=-1.0,
                  scalar2=None, op0=ALU.mult)
            for c, (off, C) in enumerate(chunks):
                mid = off + max(C // 2 - 1, 0)
                last = off + C - 1
                I("act", "activation", R=[bBW, bAUX], W=[bW[0]], out=r(W0[:, off:off + C]), in_=Bw[:, off:off + C], func=AF.Exp,
                  bias=CT_t[:, oAUX + c:oAUX + c + 1], scale=1.0)
                I("act", "activation", R=[bBW], W=[bW[1]], out=r(W1[:, off:off + C]), in_=Bw[:, off:off + C], func=AF.Exp,
                  bias=Bw[:, mid:mid + 1], scale=-1.0)
                I("act", "activation", R=[bBW], W=[bW[3]], out=r(W3[:, off:off + C]), in_=Bw[:, off:off + C], func=AF.Exp,
                  bias=Bw[:, last:last + 1], scale=-1.0)
                I("act", "activation", R=[bBW], W=[bAUX], out=CT_t[:, oAUX + 8 + c:oAUX + 9 + c], in_=Bw[:, last:last + 1], func=AF.Exp)
            I("act", "activation", R=[bBW], W=[bW[2]], out=r(W2[:, :ncol]), in_=Bw[:, :ncol], func=AF.Exp)
            I("dve", "tensor_tensor", R=[yQS, bW[0]], W=[bW[0]], out=r(W0[:, :ncol]), in0=QS[:, :ncol], in1=W0[:, :ncol], op=ALU.mult)
            I("pool", "tensor_tensor", R=[yKH, bW[1]], W=[bW[1]], out=r(W1[:, :ncol]), in0=KH[:, :ncol], in1=W1[:, :ncol], op=ALU.mult)
            I("dve", "tensor_tensor", R=[yQS, bW[2]], W=[bW[2]], out=r(W2[:, :ncol]), in0=QS[:, :ncol], in1=W2[:, :ncol], op=ALU.mult)
            I("pool", "tensor_tensor", R=[yKH, bW[3]], W=[bW[3]], out=r(W3[:, :ncol]), in0=KH[:, :ncol], in1=W3[:, :ncol], op=ALU.mult)
            yield

            def stB(c):
                off, C = chunks[c]
                par = c % 2
                VT, KD, PT = cr(oVT[par], oVT[par] + 128, C), cr(oKD[par], oKD[par] + 128, C), cr(oPT[par], oPT[par] + C, C)
                pi = nps()
                I("pe", "transpose", R=[yIF, CSb], W=[pi], out=pi.t[:C, :128], in_=IF[:, off:off + C], identity=ident)
                I("act", "activation", R=[pi], W=[bVT[par]], out=r(VT), in_=pi.t[:C, :128], func=AF.Copy)
                pk = nps()
                I("pe", "transpose", R=[bW[3], CSb], W=[pk], out=pk.t[:C, :128], in_=W3[:, off:off + C], identity=ident)
                I("act", "activation", R=[pk], W=[bKD[par]], out=r(KD), in_=pk.t[:C, :128], func=AF.Copy)
                p3 = nps()
                I("pe", "matmul", R=[bW[0], bW[1]], W=[p3], out=p3.t[:C, :C], lhsT=r(W1[:, off:off + C]), rhs=r(W0[:, off:off + C]),
                  start=True, stop=True)
                I("dve", "tensor_tensor", R=[p3, CSb], W=[bPT[par]], out=r(PT), in0=p3.t[:C, :C], in1=caus[:C, :C], op=ALU.mult)

            def stC(c):
                off, C = chunks[c]
                par = c % 2
                VT, KD, PT = cr(oVT[par], oVT[par] + 128, C), cr(oKD[par], oKD[par] + 128, C), cr(oPT[par], oPT[par] + C, C)
                pa = nps()
                I("pe", "matmul", R=[bPT[par], bVT[par]], W=[pa], inc=False, out=pa.t[:C, :128], lhsT=r(PT), rhs=r(VT),
                  start=True, stop=False)
                I("pe", "matmul", R=[bW[2], SHb[hh]], W=[pa], out=pa.t[:C, :128], lhsT=r(W2[:, off:off + C]), rhs=r(sh),
                  start=False, stop=True)
                pS = nps()
                I("pe", "matmul", R=[bKD[par], bVT[par]], W=[pS], out=pS.t[:, :128], lhsT=r(KD), rhs=r(VT), start=True, stop=True)
                I("dve", "scalar_tensor_tensor", R=[pS, bAUX, SHb[hh]], W=[SHb[hh]], out=r(sh), in0=sh,
                  scalar=CT_t[:, oAUX + 8 + c:oAUX + 9 + c], in1=pS.t[:, :128], op0=ALU.mult, op1=ALU.add)
                hg_norm(C, pa, par)

            def stD(c):
                off, C = chunks[c]
                hg_gate(hh, hs, off, C, c % 2)
            for k in range(nch + 2):
                if k < nch:
                    stB(k)
                if 0 <= k - 1 < nch:
                    stC(k - 1)
                if 0 <= k - 2 < nch:
                    stD(k - 2)
                yield

        XS_v = CR_t

        bVM = [Buf("cVM0"), Buf("cVM1")]

        def sample_ret(g, tl, gam):
            S.retire(SSh, SSb)
            S.retire(XSB, bVM)
            so = tl["soff"]
            Ktm, Vtm = XS_v[:16, 0:256], XS_v[:16, 256:512]
            VMs = [XS_v[:16, 512:768], XS_v[:16, 768:1024]]
            pk = nps()
            for e in range(2):
                I("pe", "transpose", R=[bKR[e], CSb], W=[pk], inc=(e == 1), out=pk.t[:16, e * 128:(e + 1) * 128],
                  in_=KR[:, e, so:so + 16], identity=ident)
            I("act", "activation", R=[pk], W=XSB, out=r(Ktm), in_=pk.t[:16, :256], func=AF.Copy)
            pv = nps()
            for e in range(2):
                I("pe", "transpose", R=[bVF[e], CSb], W=[pv], inc=(e == 1), out=pv.t[:16, e * 128:(e + 1) * 128],
                  in_=VF[:, e, so:so + 16], identity=ident)
            I("act", "activation", R=[pv], W=XSB, out=r(Vtm), in_=pv.t[:16, :256], func=AF.Copy)
            for e in range(2):
                I("dve", "tensor_tensor", R=[bQR[e], CSb], W=[QMb], out=QM_t[:, e, :].rearrange("p (a b) -> p a b", a=16),
                  in0=QR[:, e, so:so + 16].unsqueeze(1).broadcast_to([128, 16, 16]), in1=sel3, op=ALU.mult)
            po = PO
            yield
            sss = []

            def upd(s):
                ss = SSb[sst[0] % 2]
                sst[0] += 1
                sss.append(ss)
                row = (s * HR + g) * 128
                VM, bv = VMs[s % 2], bVM[s % 2]
                DMA(ss.t[:], sret_in[row:row + 128, :], ss, W=[ss])
                I("dve", "tensor_scalar", R=XSB + [CSb], W=[bv], out=r(VM), in0=Vtm, scalar1=CS_t[:16, C_OHK + s:C_OHK + s + 1],
                  scalar2=None, op0=ALU.mult)
                for e in range(2):
                    pu = nps()
                    I("pe", "matmul", R=XSB + [bv], W=[pu], out=pu.t[:, :256], lhsT=r(XS_v[:16, e * 128:(e + 1) * 128]), rhs=r(VM),
                      start=True, stop=True)
                    I("dve", "scalar_tensor_tensor", R=[pu, ss], W=[ss], out=ss.t[:, e * 256:(e + 1) * 256],
                      in0=ss.t[:, e * 256:(e + 1) * 256], scalar=float(gam), in1=pu.t[:, :256], op0=ALU.mult, op1=ALU.add)

            def out(s):
                ss = sss[s]
                row = (s * HR + g) * 128
                DMA(sret_s[row:row + 128, :], ss.t[:], ss, R=[ss], final=True)
                for e in range(2):
                    I("pe", "matmul", R=[QMb, ss], W=[po], inc=(e == 1), out=po.t[:16, :256],
                      lhsT=QM_t[:, e, s * 16:(s + 1) * 16], rhs=ss.t[:, e * 256:(e + 1) * 256],
                      start=(s == 0 and e == 0), stop=(s == NSMP - 1 and e == 1))
            upd(0)
            for s in range(NSMP):
                if s + 1 < NSMP:
                    upd(s + 1)
                out(s)
                yield
            I("act", "activation", R=[po], W=[bO[0]], out=ct(oO[0], oO[0] + 256, 16), in_=po.t[:16, :256], func=AF.Copy)
            ret_norm(16, 0)
            yield
            ret_gate(g, so, 16, 0)
            S.retire(bVM, XSB)
            yield

        def sample_hg(hh, hs, tl):
            S.retire(SSb, SSh)
            S.retire(XSB, bVM)
            so = tl["soff"]
            QS, FS, KH, IF = ysl(hs["QS"]), ysl(hs["FS"]), ysl(hs["KH"]), ysl(hs["IF"])
            Ktm, Itm = XS_v[:16, 0:128], XS_v[:16, 256:384]
            IMs = [XS_v[:16, 512:640], XS_v[:16, 768:896]]
            pk = nps()
            I("pe", "transpose", R=[YS[hs["KH"]], CSb], W=[pk], out=pk.t[:16, :128], in_=KH[:, so:so + 16], identity=ident)
            I("act", "activation", R=[pk], W=XSB, out=r(Ktm), in_=pk.t[:16, :128], func=AF.Copy)
            pv = nps()
            I("pe", "transpose", R=[YS[hs["IF"]], CSb], W=[pv], out=pv.t[:16, :128], in_=IF[:, so:so + 16], identity=ident)
            I("act", "activation", R=[pv], W=XSB, out=r(Itm), in_=pv.t[:16, :128], func=AF.Copy)
            I("dve", "tensor_tensor", R=[YS[hs["QS"]], CSb], W=[QMb], out=QM_t[:, 0, :].rearrange("p (a b) -> p a b", a=16),
              in0=QS[:, so:so + 16].unsqueeze(1).broadcast_to([128, 16, 16]), in1=sel3, op=ALU.mult)
            po = PO
            yield
            sss = []

            def upd(s):
                ss = SSh[sst[0] % 8]
                sst[0] += 1
                sss.append(ss)
                row = (s * HH + hh) * 128
                IM, bv = IMs[s % 2], bVM[s % 2]
                DMA(ss.t[:, :128], shg_in[row:row + 128, :], ss, W=[ss])
                I("dve", "tensor_scalar", R=XSB + [CSb], W=[bv], out=r(IM), in0=Itm, scalar1=CS_t[:16, C_OH + s:C_OH + s + 1],
                  scalar2=None, op0=ALU.mult)
                pu = nps()
                I("pe", "matmul", R=XSB + [bv], W=[pu], out=pu.t[:, :128], lhsT=r(Ktm), rhs=r(IM), start=True, stop=True)
                I("dve", "scalar_tensor_tensor", R=[pu, ss, YS[hs["FS"]]], W=[ss], out=ss.t[:, :128], in0=ss.t[:, :128],
                  scalar=FS[:, so + s:so + s + 1], in1=pu.t[:, :128], op0=ALU.mult, op1=ALU.add)

            def out(s):
                ss = sss[s]
                row = (s * HH + hh) * 128
                DMA(shg_s[row:row + 128, :], ss.t[:, :128], ss, R=[ss], final=True)
                I("pe", "matmul", R=[QMb, ss], W=[po], out=po.t[:16, :128], lhsT=QM_t[:, 0, s * 16:(s + 1) * 16],
                  rhs=ss.t[:, :128], start=(s == 0), stop=(s == NSMP - 1))
            upd(0)
            upd(1)
            for s in range(NSMP):
                if s + 2 < NSMP:
                    upd(s + 2)
                out(s)
                if s % 2 == 1:
                    yield
            hg_norm(16, po, 0)
            yield
            hg_gate(hh, hs, so, 16, 0)
            S.retire(bVM, XSB)
            yield

        def evac_y(k, e_off, ybufs, func=AF.Copy):
            def f(ps, hf, c0, nh):
                I("act", "activation", R=[ps], W=ybufs, out=r(YA_t[:, k * N + c0:k * N + c0 + nh]), in_=ps.t[:, :nh], func=func)
            return f

        def advance(gen, k=1):
            if gen is None:
                return None
            for _ in range(k):
                try:
                    next(gen)
                except StopIteration:
                    return None
            return gen

        def drain(gen):
            while gen is not None:
                gen = advance(gen, 1)

        def hg_proj(hh, hs, n, base, bg, k=1):
            box = [bg]

            def step():
                box[0] = advance(box[0], k)
            yq, yf, yl, yk, yi, yg, yt = (YS[hs[k]] for k in ("QS", "FS", "LOGF", "KH", "IF", "GH", "TB"))
            QS, FS, LOGF, KH, IF, GH, TB = (ysl(hs[k]) for k in ("QS", "FS", "LOGF", "KH", "IF", "GH", "TB"))
            pc[0] = base
            gemm_fm(next_piece(), n, evac_y(hs["QS"], 0, [yq], AF.Silu), bg=step)
            gemm_fm(next_piece(), n, evac_y(hs["FS"], 0, [yf], AF.Sigmoid), bg=step)
            I("dve", "tensor_scalar", R=[yf, SMb], W=[yf], out=r(FS[:, :n]), in0=FS[:, :n], scalar1=OMLc[:, hh:hh + 1],
              scalar2=LBc[:, hh:hh + 1], op0=ALU.mult, op1=ALU.add)
            I("act", "activation", R=[yf], W=[yl], out=r(LOGF[:, :n]), in_=FS[:, :n], func=AF.Ln)
            I("pool", "tensor_scalar", R=[yf], W=[yk], out=r(KH[:, :n]), in0=FS[:, :n], scalar1=-1.0, scalar2=1.0,
              op0=ALU.mult, op1=ALU.add)
            gemm_fm(next_piece(), n, evac_y(hs["IF"], 0, [yi], AF.Copy), bg=step)
            gemm_fm(next_piece(), n, evac_y(hs["GH"], 0, [yg], AF.Silu), bg=step)
            gemm_fm(next_piece(), n, evac_y(hs["TB"], 0, [yt], AF.Sigmoid), bg=step)
            I("dve", "scalar_tensor_tensor", R=[yt, yg, VCb], W=[yg], out=r(GH[:, :n]), in0=TB[:, :n],
              scalar=VC_t[:, V_HG + hh:V_HG + hh + 1], in1=GH[:, :n], op0=ALU.mult, op1=ALU.mult)
            return box[0]

        def prefix_state(ti, tl, cbase):
            n = tl["n"]
            S.retire([YA], YS)
            S.retire([CTb], CXALL)
            DMA(RT_t[:].rearrange("p a b -> p (a b)"), rot[ti], RTb, W=[RTb])
            nch = len(tl["chunks"])
            hs = HSET[0]
            oBB, oEE = oBW, N
            bBB, bEE = bBW, bO[0]
            bEEl = [bO[0], bO[1], bON[0], bON[1]]
            oRS = oSTAT[0] + 9
            ids_all = []
            for g in range(HR):
                b_ = 129 + g * 20
                ids_all += [b_ + 2, b_ + 3, b_ + 4, b_ + 5, b_ + 11, b_ + 12, b_ + 16, b_ + 17]
            ids_all.append(0)
            ipos = [0]

            def take():
                k = ipos[0]
                ipos[0] += 1
                pc[0] = ids_all[k]
                return next_piece(nxt=ids_all[k + 1])
            for g in range(HR):
                base = 129 + g * 20
                srg = SR_t[:, g, :].rearrange("p (a b) -> p a b", a=2)
                for e in range(2):
                    gemm_fm(take(), n, evac_y(e, 0, [bRAW[e]]))
                rotary(KR, bKR, n)
                for e in range(2):
                    gemm_fm(take(), n, evac_y(8 + e, 0, [bVF[e]]))
                acc = [PO2, PO]
                for ci, (off, C) in enumerate(tl["chunks"]):
                    VT, KD = cr(oVT[0], oVT[0] + 256, C), cr(oKD[0], oKD[0] + 256, C)
                    dcol = C_PDEC + (cbase + ci) * 8 + g
                    pv = nps()
                    for e in range(2):
                        I("pe", "transpose", R=[bVF[e], CSb], W=[pv], inc=(e == 1), out=pv.t[:C, e * 128:(e + 1) * 128],
                          in_=VF[:, e, off:off + C], identity=ident)
                    I("act", "activation", R=[pv], W=[bVT[0]], out=r(VT), in_=pv.t[:C, :256], func=AF.Copy)
                    pk = nps()
                    for e in range(2):
                        I("pe", "transpose", R=[bKR[e], CSb], W=[pk], inc=(e == 1), out=pk.t[:C, e * 128:(e + 1) * 128],
                          in_=KR[:, e, off:off + C], identity=ident)
                    I("dve", "tensor_scalar", R=[pk, CSb], W=[bKD[0]], out=r(KD), in0=pk.t[:C, :256], scalar1=CS_t[:C, dcol:dcol + 1],
                      scalar2=None, op0=ALU.mult)
                    for e in range(2):
                        I("pe", "matmul", R=[bKD[0], bVT[0]], W=[acc[e]], out=acc[e].t[:, :256],
                          lhsT=r(CR_t[:C, oKD[0] + e * 128:oKD[0] + (e + 1) * 128]), rhs=r(VT), start=(ci == 0), stop=(ci == nch - 1))
                for e in range(2):
                    I("dve", "tensor_tensor", R=[acc[e], SRb[g]], W=[SRb[g]], out=r(srg[:, e, :]), in0=acc[e].t[:, :256],
                      in1=srg[:, e, :], op=ALU.add)
                for hh in (2 * g, 2 * g + 1):
                    hb = base + 10 + (hh - 2 * g) * 5
                    sh = SH_t[:, hh, :]
                    yf, yl, yk, yi, yt = (YS[hs[k]] for k in ("FS", "LOGF", "KH", "IF", "TB"))
                    FS, LOGF, KH, IF, KDF = (ysl(hs[k]) for k in ("FS", "LOGF", "KH", "IF", "TB"))
                    gemm_fm(take(), n, evac_y(hs["FS"], 0, [yf], AF.Sigmoid))
                    I("dve", "tensor_scalar", R=[yf, SMb], W=[yf], out=r(FS[:, :n]), in0=FS[:, :n], scalar1=OMLc[:, hh:hh + 1],
                      scalar2=LBc[:, hh:hh + 1], op0=ALU.mult, op1=ALU.add)
                    I("act", "activation", R=[yf], W=[yl], out=r(LOGF[:, :n]), in_=FS[:, :n], func=AF.Ln)
                    I("pool", "tensor_scalar", R=[yf], W=[yk], out=r(KH[:, :n]), in0=FS[:, :n], scalar1=-1.0, scalar2=1.0,
                      op0=ALU.mult, op1=ALU.add)
                    gemm_fm(take(), n, evac_y(hs["IF"], 0, [yi], AF.Copy))
                    I("dve", "tensor_tensor_scan", R=[yl, SMb], W=[bBB], out=CT_t[:, oBB:oBB + n],
                      data0=ones1[:, 0:1].broadcast_to([128, n]), data1=LOGF[:, :n], initial=0.0, op0=ALU.mult, op1=ALU.add)
                    I("act", "activation", R=[bBB], W=bEEl, out=CT_t[:, oEE:oEE + n], in_=CT_t[:, oBB:oBB + n], func=AF.Exp,
                      bias=CT_t[:, oBB + n - 1:oBB + n], scale=-1.0)
                    I("act", "activation", R=[bBB], W=[bSTT[0]], out=ct(oRS, oRS + 1), in_=CT_t[:, oBB + n - 1:oBB + n], func=AF.Exp)
                    I("dve", "tensor_tensor", R=[yk] + bEEl, W=[yt], out=r(KDF[:, :n]), in0=KH[:, :n], in1=CT_t[:, oEE:oEE + n],
                      op=ALU.mult)
                    for ci, (off, C) in enumerate(tl["chunks"]):
                        VT, KD = cr(oVT[0], oVT[0] + 128, C), cr(oKD[0], oKD[0] + 128, C)
                        pi = nps()
                        I("pe", "transpose", R=[yi, CSb], W=[pi], out=pi.t[:C, :128], in_=IF[:, off:off + C], identity=ident)
                        I("act", "activation", R=[pi], W=[bVT[0]], out=r(VT), in_=pi.t[:C, :128], func=AF.Copy)
                        pk = nps()
                        I("pe", "transpose", R=[yt, CSb], W=[pk], out=pk.t[:C, :128], in_=KDF[:, off:off + C], identity=ident)
                        I("dve", "tensor_copy", R=[pk], W=[bKD[0]], out=r(KD), in_=pk.t[:C, :128])
                        I("pe", "matmul", R=[bKD[0], bVT[0]], W=[PO], out=PO.t[:, :128], lhsT=r(KD), rhs=r(VT), start=(ci == 0),
                          stop=(ci == nch - 1))
                    I("dve", "scalar_tensor_tensor", R=[PO, bSTT[0], SHb[hh]], W=[SHb[hh]], out=r(sh), in0=sh, scalar=ct(oRS, oRS + 1),
                      in1=PO.t[:, :128], op0=ALU.mult, op1=ALU.add)
            S.retire(YS, [YA])
            S.retire(CXALL, [CTb])

        sst = [0]

        def mixer(ti, tl):
            n = tl["n"]
            conv_split[0] = 1408
            YTb = [Buf("YT%d" % k) for k in range(KC)]
            M["YTb"] = YTb
            S.retire([YA], YS)
            S.retire([CTb], CXALL)
            S.retire(HG, YTb)
            DMA(RT_t[:].rearrange("p a b -> p (a b)"), rot[ti], RTb, W=[RTb])
            last = tl is otiles[-1]
            kk = 3 if tl["smp"] else 1

            def seqg(*gens):
                for g_ in gens:
                    if g_ is not None:
                        yield from g_
            bgB = None
            prevB = None
            for g in range(HR):
                gam = 1.0 - 2.0 ** (-5.0 - g)
                base = 129 + g * 20
                box = [bgB]

                def step():
                    box[0] = advance(box[0], kk)
                pc[0] = base
                for e in range(2):
                    gemm_fm(next_piece(), n, evac_y(e, 0, [bRAW[e]]), bg=step)
                rotary(QR, bQR, n)
                for e in range(2):
                    gemm_fm(next_piece(), n, evac_y(e, 0, [bRAW[e]]), bg=step)
                drain(box[0])
                if prevB is not None and last:
                    DMA(shg_p[prevB * 128:(prevB + 1) * 128, :], SH_t[:, prevB, :], SHb[prevB], R=[SHb[prevB]], final=True)
                rotary(KR, bKR, n)
                for e in range(2):
                    gemm_fm(next_piece(), n, evac_y(8 + e, 0, [bVF[e]]))
                for e in range(2):
                    gemm_fm(next_piece(), n, evac_y(10 + e, 0, [bGR[e]], AF.Silu))
                for e in range(2):
                    gemm_fm(next_piece(), n, evac_y(12, 0, bTA, AF.Sigmoid))
                    I("dve", "tensor_tensor", R=[bGR[e]] + bTA, W=[bGR[e]], out=r(GR[:, e, :n]), in0=GR[:, e, :n], in1=TA[:, :n],
                      op=ALU.mult)
                hA, hB = 2 * g, 2 * g + 1
                rem = hg_proj(hA, HSET[0], n, base + 10, seqg(ret_chain(g, tl, gam), sample_ret(g, tl, gam) if tl["smp"] else None), kk)
                drain(rem)
                if last:
                    DMA(sret_p[g * 128:(g + 1) * 128, :], SR_t[:, g, :], SRb[g], R=[SRb[g]], final=True)
                rem = hg_proj(hB, HSET[1], n, base + 15, seqg(hg_chain(hA, HSET[0], tl), sample_hg(hA, HSET[0], tl) if tl["smp"] else None), kk)
                drain(rem)
                if last:
                    DMA(shg_p[hA * 128:(hA + 1) * 128, :], SH_t[:, hA, :], SHb[hA], R=[SHb[hA]], final=True)
                bgB = seqg(hg_chain(hB, HSET[1], tl), sample_hg(hB, HSET[1], tl) if tl["smp"] else None)
                prevB = hB
            drain(bgB)
            if last:
                DMA(shg_p[prevB * 128:(prevB + 1) * 128, :], SH_t[:, prevB, :], SHb[prevB], R=[SHb[prevB]], final=True)
            S.retire(YS, [YA])
            S.retire(CXALL, [CTb])
            pc[0] = 129 + 160
            for m in range(KC):
                def ev_out(ps, hf, c0, nh, m=m):
                    I("dve", "scalar_tensor_tensor", R=[ps, XA], W=[YA], out=r(YAv[:, m, c0:c0 + nh]), in0=XA_t[:, m, c0:c0 + nh],
                      scalar=ALPHA, in1=ps.t[:, :nh], op0=ALU.mult, op1=ALU.add)
                gemm_fm(next_piece(), n, ev_out, xbufs=YTb, xt=YT)
            S.retire(YTb, HG)
            conv_split[0] = 1024
            layer_norm(n, 1)

        cb = 0
        for ti, tl in enumerate(ptiles):
            pc[0] = 0
            load_x(tl)
            ffn_ln(tl["n"], 0, after=131)
            prefix_state(ti, tl, cb)
            cb += len(tl["chunks"])
        for g in range(HR):
            I("dve", "tensor_scalar", R=[SRb[g], SELb], W=[SRb[g]], out=r(SR_t[:, g, :]), in0=SR_t[:, g, :], scalar1=SEL_t[:, 0:1],
              scalar2=None, op0=ALU.mult)
        for g in range(HH):
            I("pool", "tensor_scalar", R=[SHb[g], SELb], W=[SHb[g]], out=r(SH_t[:, g, :]), in0=SH_t[:, g, :], scalar1=SEL_t[:, 0:1],
              scalar2=None, op0=ALU.mult)
        for ti, tl in enumerate(otiles):
            pc[0] = 0
            load_x(tl)
            if stop_stage >= 1:
                ffn_ln(tl["n"], 0)
            pc[0] = 129
            if stop_stage >= 2:
                mixer(len(ptiles) + ti, tl)
            pc[0] = 129 + 176
            if stop_stage >= 3:
                ffn_ln(tl["n"], 2, after=(0 if ti + 1 < len(otiles) else None))
            store_y(tl)
        S.finish()

        with nc.Block() as block:
            @block.sync
            def _(h):
                S.replay("sp", h)

            @block.tensor
            def _(h):
                S.replay("pe", h)

            @block.scalar
            def _(h):
                S.replay("act", h)

            @block.vector
            def _(h):
                S.replay("dve", h)

            @block.gpsimd
            def _(h):
                S.replay("pool", h)
    return nc, (ptiles, otiles), NTOT, NMAX


def _fm_piece(W, cols):
    return np.ascontiguousarray(W[:, cols].reshape(KC, 128, 128).transpose(1, 0, 2)).reshape(128, 2048)


def _down_piece(Wd, k0, g, c0, ncols):
    blk = Wd[k0 * 128:(k0 + g) * 128, c0:c0 + ncols]
    return np.ascontiguousarray(blk.reshape(g, 128, ncols).transpose(1, 0, 2)).reshape(128, 2048)


def _ffn_pieces(out, p, wg, wu, wd):
    k0 = 0
    for g in GROUPS:
        for jj in range(g):
            cols = np.arange((k0 + jj) * 128, (k0 + jj + 1) * 128)
            out[p] = _fm_piece(wg, cols); p += 1
            out[p] = _fm_piece(wu, cols); p += 1
        ncols = 2048 // g
        for dp in range(g):
            out[p] = _down_piece(wd, k0, g, dp * ncols, ncols); p += 1
        k0 += g
    return p


def make_wall(i):
    wall = np.empty((NPIECE, 128, 2048), np.float32)
    p = _ffn_pieces(wall, 0, i["ffn1_w_gate"][0], i["ffn1_w_up"][0], i["ffn1_w_down"][0])
    win = i["w_in"][0]
    a128 = np.arange(128)
    for g in range(HR):
        base = g * 256
        for proj in (0, 1):
            for e in range(2):
                wall[p] = _fm_piece(win, proj * D + base + 2 * a128 + e); p += 1
        for proj in (2, 3, 8):
            for e in range(2):
                wall[p] = _fm_piece(win, proj * D + base + e * 128 + a128); p += 1
        for hh in (2 * g, 2 * g + 1):
            for proj in (4, 5, 6, 7, 9):
                wall[p] = _fm_piece(win, proj * D + hh * 128 + a128); p += 1
    wo = i["w_out"][0]
    for m in range(KC):
        wall[p] = _fm_piece(wo, m * 128 + a128); p += 1
    p = _ffn_pieces(wall, p, i["ffn2_w_gate"][0], i["ffn2_w_up"][0], i["ffn2_w_down"][0])
    assert p == NPIECE
    return wall


def make_consts():
    c = np.zeros((128, C_END), np.float64)
    c[:, C_ID:C_ID + 128] = np.eye(128, dtype=np.float32)
    j = np.arange(128)[:, None].astype(np.float64)
    i = np.arange(128)[None, :].astype(np.float64)
    c[:, C_CAUS:C_CAUS + 128] = (i >= j)
    for h in range(HR):
        gam = 1.0 - 2.0 ** (-5.0 - h)
        c[:, C_DM + h * 128:C_DM + (h + 1) * 128] = np.where(i >= j, gam ** np.maximum(i - j, 0), 0.0) / 16.0
        c[:, C_KD128 + h] = gam ** (127.0 - np.arange(128)) / 16.0
        c[:PRE, C_KD16 + h] = gam ** (PRE - 1.0 - np.arange(PRE)) / 16.0
        c[:, C_RD + h] = gam ** (np.arange(128) + 1.0)
    sel = np.zeros((16, 16), np.float32)
    sel[np.arange(16), np.arange(16)] = 1.0
    c[:, C_SEL:C_SEL + 256] = sel.reshape(1, 256)
    c[:16, C_OH:C_OH + 16] = np.eye(16, dtype=np.float32)
    c[:16, C_OHK:C_OHK + 16] = np.eye(16, dtype=np.float32) / 16.0
    return c


def add_prefix_decay(c, ptiles):
    total = ptiles[-1]["g0"] + ptiles[-1]["n"]
    slot = 0
    for tl in ptiles:
        for (off, C) in tl["chunks"]:
            gpos = tl["g0"] + off + np.arange(C)
            for h in range(HR):
                gam = 1.0 - 2.0 ** (-5.0 - h)
                c[:C, C_PDEC + slot * 8 + h] = gam ** (total - 1.0 - gpos) / 16.0
            slot += 1
    assert slot <= 16
    return c


def make_rot(ptiles, otiles, part, NMAX):
    tiles = ptiles + otiles
    nt = len(tiles)
    half = ptiles[-1]["g0"] + ptiles[-1]["n"]
    rot = np.zeros((nt, 128, 2, NMAX), np.float32)
    inv = (10000.0 ** (-np.arange(0, 256, 2, dtype=np.float32) / np.float32(256))).astype(np.float32)
    for ti, tl in enumerate(tiles):
        base = tl["g0"] + (part * half if ti >= len(ptiles) else 0)
        pos = (base + np.arange(tl["n"])).astype(np.float32)
        if tl["smp"]:
            pos[tl["soff"]:tl["soff"] + 16] = PAST
        ang = (pos[None, :] * inv[:, None]).astype(np.float32)
        rot[ti, :, 0, :tl["n"]] = np.cos(ang).astype(np.float32)
        rot[ti, :, 1, :tl["n"]] = np.sin(ang).astype(np.float32)
    return rot.reshape(nt, 128, 2 * NMAX)


def colvec(v):
    return np.ascontiguousarray(np.asarray(v, np.float32).reshape(KC, 128).T)


_CACHE = {}


def kernel(**inputs):
    i = {k: np.asarray(v) for k, v in inputs.items()}
    B, SEQ, _ = i["x_prompt"].shape
    nmain = (SEQ + NMETA) // 2 // 128
    stop_stage = int(_CACHE.get("stop_stage", 99))
    key = (nmain, stop_stage)
    if key not in _CACHE:
        _CACHE[key] = build(nmain, stop_stage)
    nc, (ptiles, otiles), NTOT, NMAX = _CACHE[key]
    half = ptiles[-1]["g0"] + ptiles[-1]["n"]
    assert 2 * half == SEQ + NMETA
    wall = make_wall(i)
    cst = add_prefix_decay(make_consts(), ptiles).astype(np.float32)
    rots = [make_rot(ptiles, otiles, part, NMAX) for part in range(2)]
    vecs = np.zeros((128, V_END), np.float32)
    for li, (gk, bk) in enumerate((("ln1_g", "ln1_b"), ("ln2_g", "ln2_b"), ("ln3_g", "ln3_b"))):
        vecs[:, V_LN + li * 32:V_LN + li * 32 + 16] = colvec(i[gk][0])
        vecs[:, V_LN + li * 32 + 16:V_LN + li * 32 + 32] = colvec(i[bk][0])
    vecs[:, V_LB0:V_LB0 + 16] = colvec(i["hgrn_lb_logits"][0])
    vecs[:, V_LB1:V_LB1 + 16] = colvec(i["hgrn_lb_logits"][1])
    vecs[:, V_HG:V_HG + 16] = colvec(i["hgrn_norm_g"][0])
    in_maps = []
    for c in range(8):
        b, part = c // 2, c % 2
        full = np.concatenate([i["meta_tokens"], i["x_prompt"][b]], axis=0)
        prefix = np.zeros((half, D), np.float32) if part == 0 else full[:half]
        xin = np.concatenate([prefix, full[part * half:(part + 1) * half], i["x_sample"][16 * c:16 * c + 16, 0, :]], axis=0)
        in_maps.append(dict(
            xin=np.ascontiguousarray(xin, np.float32), wall=wall, rot=rots[part], vecs=vecs, cst=cst,
            selv=np.full((128, 1), float(part), np.float32),
            sret_in=np.ascontiguousarray(i["state_ret"][0, 16 * c:16 * c + 16]).reshape(NSMP * HR * 128, 512),
            shg_in=np.ascontiguousarray(i["state_hgrn"][0, 16 * c:16 * c + 16]).reshape(NSMP * HH * 128, 128)))
    res = run_bass_kernel_spmd(nc, in_maps, core_ids=list(range(8)))
    R = res.results
    y_prompt = np.stack([np.concatenate([R[2 * b]["y"][:half], R[2 * b + 1]["y"][:half]], axis=0)[NMETA:] for b in range(B)], axis=0)
    y_sample = np.concatenate([R[c]["y"][half:half + 16] for c in range(8)], axis=0)[:, None, :]
    srp = np.stack([R[2 * b + 1]["sret_p"].reshape(HR, 256, 256) for b in range(B)], axis=0)[None]
    srs = np.concatenate([R[c]["sret_s"].reshape(NSMP, HR, 256, 256) for c in range(8)], axis=0)[None]
    shp = np.stack([R[2 * b + 1]["shg_p"].reshape(HH, 128, 128) for b in range(B)], axis=0)[None]
    shs = np.concatenate([R[c]["shg_s"].reshape(NSMP, HH, 128, 128) for c in range(8)], axis=0)[None]
    return (y_prompt.astype(np.float32), y_sample.astype(np.float32), srp.astype(np.float32),
            srs.astype(np.float32), shp.astype(np.float32), shs.astype(np.float32))
```

```python
import contextlib
import numpy as np
import concourse.bass as bass
import concourse.mybir as mybir
from concourse.bass_utils import run_bass_kernel_spmd

F32 = mybir.dt.float32
F32R = mybir.dt.float32r
AF = mybir.ActivationFunctionType
ALU = mybir.AluOpType

D = 2048
KC = 16
FF = 5504
FFC = 43
NMETA = 16
HR = 8
HH = 16
PAST = 16384
ALPHA = 2.0 ** 0.25
EPS = 1e-5
NPIECE = 434
GROUPS = [8, 8, 8, 8, 8, 2, 1]
NSMP = 16


class Ev:
    __slots__ = ("sem", "val", "eng")

    def __init__(self, sem, val, eng):
        self.sem, self.val, self.eng = sem, val, eng


class Buf:
    def __init__(self, name, t=None):
        self.name = name
        self.t = t
        self.w = None
        self.r = {}
        self.dsem = None
        self.dcnt = 0


class Eng:
    def __init__(self, name, sem):
        self.name = name
        self.sem = sem
        self.cnt = 0
        self.prog = []
        self.waited = {}


class Sched:
    def __init__(self, nc, stack):
        self.nc = nc
        self.stack = stack
        self.engs = {}
        for n in ("pe", "act", "dve", "pool", "sp"):
            self.engs[n] = Eng(n, stack.enter_context(nc.semaphore("sem_" + n)))
        self.dsems = []
        self.final = []

    def _deps(self, E, R, W):
        deps = []
        for b in R:
            if b.w is not None:
                deps.append(b.w)
        for b in W:
            if b.w is not None:
                deps.append(b.w)
            deps.extend(b.r.values())
        for ev in deps:
            if ev.eng is E and E.name == "pe":
                continue
            k = id(ev.sem)
            if E.waited.get(k, 0) < ev.val:
                E.prog.append(("w", ev.sem, ev.val))
                E.waited[k] = ev.val

    @staticmethod
    def _mark(ev, R, W):
        for b in R:
            k = id(ev.sem)
            o = b.r.get(k)
            if o is None or o.val < ev.val:
                b.r[k] = ev
        for b in W:
            b.w = ev
            b.r = {}

    def I(self, en, _opn, R=(), W=(), inc=True, **kw):
        fn = (_opn, kw)
        E = self.engs[en]
        self._deps(E, R, W)
        if inc:
            E.cnt += 1
            ev = Ev(E.sem, E.cnt, E)
        else:
            ev = Ev(E.sem, E.cnt + 1, E)
        E.prog.append(("i", fn, inc))
        self._mark(ev, R, W)
        return ev

    def DMA(self, out, in_, sb, R=(), W=(), final=False):
        E = self.engs["sp"]
        self._deps(E, R, W)
        if sb.dsem is None:
            sb.dsem = self.stack.enter_context(self.nc.semaphore("dsem_%d" % len(self.dsems)))
            self.dsems.append(sb.dsem)
        sb.dcnt += 16
        ev = Ev(sb.dsem, sb.dcnt, None)
        E.prog.append(("d", out, in_, sb.dsem))
        self._mark(ev, R, W)
        if final:
            self.final.append(ev)
        return ev

    def retire(self, old, new):
        evs = []
        for b in old:
            if b.w is not None:
                evs.append(b.w)
            evs.extend(b.r.values())
        for b in new:
            for ev in evs:
                k = id(ev.sem)
                o = b.r.get(k)
                if o is None or o.val < ev.val:
                    b.r[k] = ev

    def finish(self):
        E = self.engs["sp"]
        for ev in self.final:
            k = id(ev.sem)
            if E.waited.get(k, 0) < ev.val:
                E.prog.append(("w", ev.sem, ev.val))
                E.waited[k] = ev.val

    def replay(self, en, h):
        E = self.engs[en]
        for it in E.prog:
            if it[0] == "w":
                h.wait_ge(it[1], it[2])
            elif it[0] == "i":
                ins = getattr(h, it[1][0])(**it[1][1])
                if it[2]:
                    ins.then_inc(E.sem, 1)
            else:
                h.dma_start(out=it[1], in_=it[2]).then_inc(it[3], 16)


PRE = 8


def tile_plan(nmain):
    nt = nmain // 4
    out = []
    col = 0
    for part in range(2):
        tiles = []
        for t in range(nt):
            pre = t == 0
            smp = (part == 1) and (t == nt - 1)
            n = 512 + (PRE if pre else 0) + (16 if smp else 0)
            chunks = []
            o = 0
            if pre:
                chunks.append((0, PRE))
                o = PRE
            for i in range(4):
                chunks.append((o + 128 * i, 128))
            soff = o + 512
            g0 = 0 if t == 0 else PRE + 512 * t
            tiles.append(dict(c0=col, n=n, pre=pre, smp=smp, chunks=chunks, soff=soff, g0=g0))
            col += n
        out.append(tiles)
    return out[0], out[1], col


C_ID = 0
C_CAUS = 128
C_DM = 256
C_KD128 = C_DM + 8 * 128
C_KD16 = C_KD128 + 8
C_RD = C_KD16 + 8
C_SEL = C_RD + 8
C_OH = C_SEL + 256
C_OHK = C_OH + 16
C_PDEC = C_OHK + 16
C_END = C_PDEC + 128
V_LN = 0
V_LB0 = 96
V_LB1 = 112
V_HG = 128
V_END = 144


def build(nmain, stop_stage=99):
    ptiles, otiles, NTOT = tile_plan(nmain)
    tiles = ptiles + otiles
    NT = len(tiles)
    NOWN = NTOT - otiles[0]["c0"]
    NMAX = max(t["n"] for t in tiles)
    N = NMAX
    nc = bass.Bass("TRN2", target_bir_lowering=False)
    dr = lambda name, shape, kind: nc.dram_tensor(name, shape, F32, kind=kind).ap()
    xin = dr("xin", [NTOT, D], "ExternalInput")
    wall = dr("wall", [NPIECE, 128, 2048], "ExternalInput")
    rot = dr("rot", [NT, 128, 2 * NMAX], "ExternalInput")
    vecs = dr("vecs", [128, V_END], "ExternalInput")
    cst = dr("cst", [128, C_END], "ExternalInput")
    sret_in = dr("sret_in", [NSMP * HR * 128, 512], "ExternalInput")
    shg_in = dr("shg_in", [NSMP * HH * 128, 128], "ExternalInput")
    yout = dr("y", [NOWN, D], "ExternalOutput")
    selv = dr("selv", [128, 1], "ExternalInput")
    sret_p = dr("sret_p", [HR * 128, 512], "ExternalOutput")
    shg_p = dr("shg_p", [HH * 128, 128], "ExternalOutput")
    sret_s = dr("sret_s", [NSMP * HR * 128, 512], "ExternalOutput")
    shg_s = dr("shg_s", [NSMP * HH * 128, 128], "ExternalOutput")

    with contextlib.ExitStack() as stack:
        S = Sched(nc, stack)
        I, DMA = S.I, S.DMA

        def sb(name, shape):
            return stack.enter_context(nc.sbuf_tensor(name, shape, F32))

        XA_t = sb("XA", [128, KC, NMAX])
        YA_t = sb("YA", [128, KC * NMAX])
        HG_t = sb("HG", [128, KC, NMAX])
        ST_t = [sb("ST%d" % i, [128, 2048]) for i in range(3)]
        WC_t = [sb("WC%d" % i, [128, 2048]) for i in range(2)]
        SR_t = sb("SR", [128, HR, 512])
        SH_t = sb("SH", [128, HH, 128])
        CT_t = sb("CT", [128, 2176])
        CR_t = sb("CR", [128, 4 * NMAX + 1280])
        ON1_t = sb("ON1", [128, 128])
        CS_t = sb("CS", [128, C_END])
        RT_t = sb("RT", [128, 2, NMAX])
        VC_t = sb("VC", [128, V_END])
        SM_t = sb("SM", [128, 512])
        SS_t = [sb("SS%d" % i, [128, 512]) for i in range(2)]
        QM_t = sb("QM", [128, 2, 256])
        PS = [Buf("ps%d" % i, stack.enter_context(nc.psum_tensor("ps%d" % i, [128, 512], F32))) for i in range(8)]
        PO = PS[7]
        PO2 = PS[6]
        ps_ctr = [0]

        def nps():
            p = PS[ps_ctr[0] % 6]
            ps_ctr[0] += 1
            return p

        def r(ap):
            return ap.bitcast(F32R)

        XA = Buf("XA", XA_t)
        YA = Buf("YA", YA_t)
        HG = [Buf("HG0"), Buf("HG1")]
        ST = [Buf("ST%d" % i, ST_t[i]) for i in range(3)]
        WC = [(Buf("WC%da" % i, WC_t[i]), Buf("WC%db" % i, WC_t[i])) for i in range(2)]
        SSb = [Buf("SS%d" % i, SS_t[i]) for i in range(2)]
        SSh = [Buf("SSh%d" % k, SS_t[k // 4][:, (k % 4) * 128:(k % 4 + 1) * 128]) for k in range(8)]
        SRb = [Buf("SR%d" % g) for g in range(HR)]
        SHb = [Buf("SH%d" % g) for g in range(HH)]
        CSb = Buf("CS", CS_t)
        RTb = Buf("RT", RT_t)
        VCb = Buf("VC", VC_t)
        SMb = Buf("SM", SM_t)
        CTb = Buf("CT", CT_t)
        QMb = Buf("QM")
        YAv = YA_t[:].rearrange("p (a b) -> p a b", a=KC)

        ident = CS_t[:, C_ID:C_ID + 128]
        caus = CS_t[:, C_CAUS:C_CAUS + 128]
        sel3 = CS_t[:, C_SEL:C_SEL + 256].rearrange("p (a b) -> p a b", a=16)
        onesD = ON1_t[:, :]
        ones1 = SM_t[:, 128:256]
        LBc = SM_t[:, 256:272]
        OMLc = SM_t[:, 272:288]
        zer = SM_t[:, 320:448]

        def epsc(C=128):
            return SM_t[:C, 288:289]

        def ct(a, b, C=128):
            return CT_t[:C, a:b]

        def cr(a, b, C=128):
            return CR_t[:C, a:b]

        DMA(CS_t[:], cst[:, :], CSb, W=[CSb])
        DMA(VC_t[:], vecs[:, :], VCb, W=[VCb])
        SEL_t = sb("SELV", [128, 1])
        SELb = Buf("SELV", SEL_t)
        DMA(SEL_t[:], selv[:, :], SELb, W=[SELb])
        I("pool", "memset", W=[SMb], ap=SM_t[:, 128:256], constant=1.0)
        I("pool", "memset", W=[SMb], ap=SM_t[:, 288:289], constant=EPS)
        I("pool", "memset", W=[SMb], ap=SM_t[:, 320:448], constant=0.0)
        I("dve", "tensor_scalar", R=[SMb], W=[SMb], out=r(onesD), in0=ones1, scalar1=1.0 / D, scalar2=None, op0=ALU.mult)
        I("dve", "tensor_tensor", R=[VCb], W=[SMb], out=SM_t[:, 296:312], in0=VC_t[:, V_LB0:V_LB0 + 16],
          in1=VC_t[:, V_LB1:V_LB1 + 16], op=ALU.subtract)
        I("act", "activation", R=[SMb], W=[SMb], out=LBc, in_=SM_t[:, 296:312], func=AF.Sigmoid)
        I("dve", "tensor_scalar", R=[SMb], W=[SMb], out=OMLc, in0=LBc, scalar1=-1.0, scalar2=1.0, op0=ALU.mult, op1=ALU.add)
        for g in range(HR):
            for q4 in range(4):
                I("pool", "tensor_copy", R=[SMb], W=[SRb[g]], out=r(SR_t[:, g, q4 * 128:(q4 + 1) * 128]), in_=zer)
        for g in range(HH):
            I("pool", "tensor_copy", R=[SMb], W=[SHb[g]], out=r(SH_t[:, g, :]), in_=zer)

        pc = [0]

        seqc = [0]
        wcc = [0]
        conv_split = [1024]

        pref = {}

        def next_piece(nxt=-1):
            p = pc[0]
            pc[0] += 1
            wc = pref.pop(p, None)
            pref.clear()
            if wc is None:
                wc = issue_piece(p)
            n2 = p + 1 if nxt == -1 else nxt
            if n2 is not None and n2 < NPIECE:
                pref[n2] = issue_piece(n2)
            return wc

        def issue_piece(p):
            q = seqc[0]
            seqc[0] += 1
            st = ST[q % 3]
            wc = WC[wcc[0] % 2]
            wcc[0] += 1
            DMA(st.t[:], wall[p % NPIECE], st, W=[st])
            sp_ = conv_split[0]
            I("act", "activation", R=[st], W=[wc[0]], out=r(wc[0].t[:, :sp_]), in_=st.t[:, :sp_], func=AF.Copy)
            I("dve", "tensor_copy", R=[st], W=[wc[1]], out=r(wc[1].t[:, sp_:]), in_=st.t[:, sp_:])
            return wc

        def halves_of(n):
            return [(0, n // 2), (n // 2, n // 2)]

        def gemm_fm(wc, n, evac, xbufs=None, xt=None, bg=None):
            xbufs = [XA] if xbufs is None else xbufs
            xt = XA_t if xt is None else xt
            wv = wc[0].t[:].rearrange("p (a b) -> p a b", a=KC)
            for hf, (c0, nh) in enumerate(halves_of(n)):
                ps = nps()
                for kc in range(KC):
                    I("pe", "matmul", R=[wc[0], wc[1]] + xbufs, W=[ps], inc=(kc == KC - 1),
                      out=ps.t[:, :nh], lhsT=r(wv[:, kc, :]), rhs=r(xt[:, kc, c0:c0 + nh]),
                      start=(kc == 0), stop=(kc == KC - 1))
                evac(ps, hf, c0, nh)
                if bg is not None:
                    bg()

        def layer_norm(n, lnidx):
            gcol = V_LN + lnidx * 32
            bcol = gcol + 16
            MEAN, MSQ, VAR, LNV, RSTD, NMR = (CT_t[:, k * 272:(k + 1) * 272] for k in range(6))
            for hf, (c0, nh) in enumerate(halves_of(n)):
                I("act", "activation", R=[YA], W=[HG[0], HG[1]], out=r(HG_t[:, :, c0:c0 + nh]), in_=YAv[:, :, c0:c0 + nh],
                  func=AF.Square)
                p1 = nps()
                for kc in range(KC):
                    I("pe", "matmul", R=[SMb, YA], W=[p1], inc=(kc == KC - 1), out=p1.t[:, :nh], lhsT=r(onesD),
                      rhs=r(YAv[:, kc, c0:c0 + nh]), start=(kc == 0), stop=(kc == KC - 1))
                p2 = nps()
                for kc in range(KC):
                    I("pe", "matmul", R=[SMb, HG[0], HG[1]], W=[p2], inc=(kc == KC - 1), out=p2.t[:, :nh], lhsT=r(onesD),
                      rhs=r(HG_t[:, kc, c0:c0 + nh]), start=(kc == 0), stop=(kc == KC - 1))
                I("act", "activation", R=[p1], W=[CTb], out=MEAN[:, :nh], in_=p1.t[:, :nh], func=AF.Copy)
                I("dve", "tensor_tensor", R=[CTb], W=[CTb], out=MSQ[:, :nh], in0=MEAN[:, :nh], in1=MEAN[:, :nh], op=ALU.mult)
                I("dve", "tensor_tensor", R=[p2, CTb], W=[CTb], out=VAR[:, :nh], in0=p2.t[:, :nh], in1=MSQ[:, :nh], op=ALU.subtract)
                I("act", "activation", R=[CTb, SMb], W=[CTb], out=LNV[:, :nh], in_=VAR[:, :nh], func=AF.Ln, bias=epsc(), scale=1.0)
                I("act", "activation", R=[CTb], W=[CTb], out=RSTD[:, :nh], in_=LNV[:, :nh], func=AF.Exp, scale=-0.5)
                I("dve", "scalar_tensor_tensor", R=[CTb], W=[CTb], out=NMR[:, :nh], in0=MEAN[:, :nh], scalar=-1.0,
                  in1=RSTD[:, :nh], op0=ALU.mult, op1=ALU.mult)
                I("dve", "tensor_tensor", R=[YA, CTb], W=[XA], out=r(XA_t[:, :, c0:c0 + nh]), in0=YAv[:, :, c0:c0 + nh],
                  in1=RSTD[:, :nh].unsqueeze(1).broadcast_to([128, KC, nh]), op=ALU.mult)
                I("dve", "tensor_tensor", R=[XA, CTb], W=[XA], out=r(XA_t[:, :, c0:c0 + nh]), in0=XA_t[:, :, c0:c0 + nh],
                  in1=NMR[:, :nh].unsqueeze(1).broadcast_to([128, KC, nh]), op=ALU.add)
                for kc in range(KC):
                    I("dve", "tensor_scalar", R=[XA, VCb], W=[XA], out=r(XA_t[:, kc, c0:c0 + nh]), in0=XA_t[:, kc, c0:c0 + nh],
                      scalar1=VC_t[:, gcol + kc:gcol + kc + 1], scalar2=VC_t[:, bcol + kc:bcol + kc + 1],
                      op0=ALU.mult, op1=ALU.add)

        def ffn_ln(n, lnidx, after=-1):
            SG = [CT_t[:, 1632:1632 + 272], CT_t[:, 1904:1904 + 272]]
            SGb = [Buf("SG0"), Buf("SG1")]
            S.retire([CTb], SGb)
            I("dve", "tensor_scalar", R=[XA], W=[YA], out=r(YAv[:, :, :n]), in0=XA_t[:, :, :n], scalar1=ALPHA, scalar2=None,
              op0=ALU.mult)
            for gi, g in enumerate(GROUPS):
                hg = HG[gi % 2]
                hgv = HG_t[:, (gi % 2) * 8:(gi % 2) * 8 + 8, :]
                for jj in range(g):
                    def ev_gate(ps, hf, c0, nh):
                        I("act", "activation", R=[ps], W=[SGb[hf]], out=SG[hf][:, :nh], in_=ps.t[:, :nh], func=AF.Silu)
                    gemm_fm(next_piece(), n, ev_gate)

                    def ev_up(ps, hf, c0, nh, jj=jj, hg=hg, hgv=hgv):
                        I("dve", "tensor_tensor", R=[ps, SGb[hf]], W=[hg], out=r(hgv[:, jj, c0:c0 + nh]), in0=ps.t[:, :nh],
                          in1=SG[hf][:, :nh], op=ALU.mult)
                    gemm_fm(next_piece(), n, ev_up)
                ncols = 2048 // g
                for dp in range(g):
                    wd = next_piece(nxt=after) if (gi == len(GROUPS) - 1 and dp == g - 1) else next_piece()
                    wv = wd[0].t[:].rearrange("p (a b) -> p a b", a=g)
                    for mt in range(ncols // 128):
                        m = dp * (ncols // 128) + mt
                        for hf, (c0, nh) in enumerate(halves_of(n)):
                            ps = nps()
                            for kc in range(g):
                                I("pe", "matmul", R=[wd[0], wd[1], hg], W=[ps], inc=(kc == g - 1), out=ps.t[:, :nh],
                                  lhsT=r(wv[:, kc, mt * 128:(mt + 1) * 128]), rhs=r(hgv[:, kc, c0:c0 + nh]),
                                  start=(kc == 0), stop=(kc == g - 1))
                            I("dve", "scalar_tensor_tensor", R=[ps, YA], W=[YA], out=r(YAv[:, m, c0:c0 + nh]), in0=ps.t[:, :nh],
                              scalar=0.5, in1=YAv[:, m, c0:c0 + nh], op0=ALU.mult, op1=ALU.add)
            S.retire(SGb, [CTb])
            layer_norm(n, lnidx)

        def blocks_of(tl):
            bl = list(tl["chunks"])
            if tl["smp"]:
                bl.append((tl["soff"], 16))
            return bl

        io_ctr = [0]

        def load_x(tl):
            for (off, C) in blocks_of(tl):
                io = ST[seqc[0] % 3]
                seqc[0] += 1
                DMA(io.t[:C, :], xin[tl["c0"] + off:tl["c0"] + off + C, :], io, W=[io])
                for k4 in range(4):
                    ps = nps()
                    for q in range(4):
                        kc = k4 * 4 + q
                        I("pe", "transpose", R=[io, CSb], W=[ps], inc=(q == 3), out=ps.t[:, q * 128:q * 128 + C],
                          in_=io.t[:C, kc * 128:(kc + 1) * 128], identity=ident[:C, :C])
                    I("act", "activation", R=[ps], W=[XA], out=r(XA_t[:, k4 * 4:k4 * 4 + 4, off:off + C]),
                      in_=ps.t[:].rearrange("p (a b) -> p a b", a=4)[:, :, :C], func=AF.Copy)

        def store_y(tl):
            for (off, C) in blocks_of(tl):
                io = ST[seqc[0] % 3]
                seqc[0] += 1
                for k4 in range(4):
                    ps = nps()
                    for q in range(4):
                        kc = k4 * 4 + q
                        I("pe", "transpose", R=[XA, CSb], W=[ps], inc=(q == 3), out=ps.t[:C, q * 128:(q + 1) * 128],
                          in_=XA_t[:, kc, off:off + C], identity=ident)
                    I("dve", "tensor_copy", R=[ps], W=[io], out=io.t[:C, k4 * 512:(k4 + 1) * 512], in_=ps.t[:C, :])
                DMA(yout[tl["c0"] - otiles[0]["c0"] + off:tl["c0"] - otiles[0]["c0"] + off + C, :], io.t[:C, :], io, R=[io], final=True)

        YS = [Buf("YS%d" % k) for k in range(16)]

        def ysl(k, w=1):
            return YA_t[:, k * N:(k + w) * N]

        def y3(k):
            return YA_t[:, k * N:(k + 2) * N].rearrange("p (a b) -> p a b", a=2)
        RAW, QR, KR, VF, GR = y3(0), y3(2), y3(4), y3(8), y3(10)
        bRAW, bQR, bKR, bVF, bGR = [YS[0], YS[1]], [YS[2], YS[3]], [YS[4], YS[5]], [YS[8], YS[9]], [YS[10], YS[11]]
        T1, T2, TA = ysl(6), ysl(7), ysl(12)
        bT1, bT2, bTA = [YS[6]], [YS[7]], [YS[12]]
        HSET = [dict(QS=0, FS=1, LOGF=6, KH=7, TB=12, IF=13, GH=14),
                dict(QS=4, FS=5, LOGF=8, KH=9, IF=10, GH=11, TB=15)]
        YT = HG_t
        COS, SIN = RT_t[:, 0, :], RT_t[:, 1, :]
        oW = [0, N, 2 * N, 3 * N]
        oVT = [4 * N, 4 * N + 256]
        oKD = [4 * N + 512, 4 * N + 768]
        oPT = [4 * N + 1024, 4 * N + 1152]
        oBW = 0
        oO = [N, N + 256]
        oON = [N + 512, N + 768]
        oTBS = [N + 1024, N + 1280]
        oSTAT = [N + 1536, N + 1552]
        oAUX = N + 1568
        bW = [Buf("cW%d" % k) for k in range(4)]
        bVT, bKD, bPT = ([Buf(nm + str(k)) for k in range(2)] for nm in ("cVT", "cKD", "cPT"))
        bO, bON, bTBS, bSTT = ([Buf(nm + str(k)) for k in range(2)] for nm in ("cO", "cON", "cTBS", "cSTT"))
        bBW, bAUX = Buf("cBW"), Buf("cAUX")
        CXALL = bW + bVT + bKD + bPT + bO + bON + bTBS + bSTT + [bBW, bAUX]
        XSB = [bW[0], bW[1]]
        M = dict(YTb=None)

        def rstd_from(var_ap, C, par):
            o = oSTAT[par]
            I("act", "activation", R=[bSTT[par], SMb], W=[bSTT[par]], out=ct(o + 8, o + 9, C), in_=var_ap, func=AF.Ln, bias=epsc(C), scale=1.0)
            I("act", "activation", R=[bSTT[par]], W=[bSTT[par]], out=ct(o + 9, o + 10, C), in_=ct(o + 8, o + 9, C), func=AF.Exp, scale=-0.5)

        def ret_norm(C, par):
            o = oSTAT[par]
            I("dve", "bn_stats", R=[bO[par]], W=[bSTT[par]], out=ct(o, o + 6, C), in_=ct(oO[par], oO[par] + 256, C))
            I("dve", "bn_aggr", R=[bSTT[par]], W=[bSTT[par]], out=ct(o + 6, o + 8, C), in_=ct(o, o + 6, C))
            rstd_from(ct(o + 7, o + 8, C), C, par)
            I("dve", "tensor_scalar", R=[bO[par], bSTT[par]], W=[bON[par]], out=ct(oON[par], oON[par] + 256, C),
              in0=ct(oO[par], oO[par] + 256, C), scalar1=ct(o + 6, o + 7, C), scalar2=ct(o + 9, o + 10, C),
              op0=ALU.subtract, op1=ALU.mult)

        def ret_gate(g, off, C, par):
            YTb = M["YTb"]
            for e in range(2):
                pt = nps()
                I("pe", "transpose", R=[bON[par], CSb], W=[pt], out=pt.t[:, :C],
                  in_=ct(oON[par] + e * 128, oON[par] + (e + 1) * 128, C), identity=ident[:C, :C])
                I("dve", "tensor_tensor", R=[pt, bGR[e]], W=[YTb[2 * g + e]], out=r(YT[:, 2 * g + e, off:off + C]), in0=pt.t[:, :C],
                  in1=GR[:, e, off:off + C], op=ALU.mult)

        def hg_norm(C, pa, par):
            o = oSTAT[par]
            I("act", "activation", R=[pa], W=[bTBS[par], bSTT[par]], out=ct(oTBS[par], oTBS[par] + 128, C), in_=pa.t[:C, :128],
              func=AF.Square, accum_out=ct(o + 10, o + 11, C))
            I("dve", "tensor_scalar", R=[bSTT[par]], W=[bSTT[par]], out=ct(o + 11, o + 12, C), in0=ct(o + 10, o + 11, C),
              scalar1=1.0 / 128, scalar2=None, op0=ALU.mult)
            rstd_from(ct(o + 11, o + 12, C), C, par)
            I("dve", "tensor_scalar", R=[pa, bSTT[par]], W=[bON[par]], out=ct(oON[par], oON[par] + 128, C), in0=pa.t[:C, :128],
              scalar1=ct(o + 9, o + 10, C), scalar2=None, op0=ALU.mult)

        def hg_gate(hh, hs, off, C, par):
            YTb = M["YTb"]
            GH = ysl(hs["GH"])
            pt = nps()
            I("pe", "transpose", R=[bON[par], CSb], W=[pt], out=pt.t[:, :C], in_=ct(oON[par], oON[par] + 128, C), identity=ident[:C, :C])
            I("dve", "tensor_tensor", R=[pt, YS[hs["GH"]]], W=[bTBS[par]], out=CT_t[:, oTBS[par]:oTBS[par] + C], in0=pt.t[:, :C],
              in1=GH[:, off:off + C], op=ALU.mult)
            I("dve", "tensor_tensor", R=[bTBS[par], YTb[hh]], W=[YTb[hh]], out=r(YT[:, hh, off:off + C]),
              in0=CT_t[:, oTBS[par]:oTBS[par] + C], in1=YT[:, hh, off:off + C], op=ALU.add)

        def rotary(dst, bdst, n):
            c_, s_ = COS, SIN
            I("dve", "tensor_tensor", R=[bRAW[0], RTb], W=bT1, out=r(T1[:, :n]), in0=RAW[:, 0, :n], in1=c_[:, :n], op=ALU.mult)
            I("pool", "tensor_tensor", R=[bRAW[1], RTb], W=bT2, out=r(T2[:, :n]), in0=RAW[:, 1, :n], in1=s_[:, :n], op=ALU.mult)
            I("dve", "tensor_tensor", R=bT1 + bT2, W=[bdst[0]], out=r(dst[:, 0, :n]), in0=T1[:, :n], in1=T2[:, :n], op=ALU.subtract)
            I("dve", "tensor_tensor", R=[bRAW[0], RTb], W=bT1, out=r(T1[:, :n]), in0=RAW[:, 0, :n], in1=s_[:, :n], op=ALU.mult)
            I("pool", "tensor_tensor", R=[bRAW[1], RTb], W=bT2, out=r(T2[:, :n]), in0=RAW[:, 1, :n], in1=c_[:, :n], op=ALU.mult)
            I("dve", "tensor_tensor", R=bT1 + bT2, W=[bdst[1]], out=r(dst[:, 1, :n]), in0=T1[:, :n], in1=T2[:, :n], op=ALU.add)

        def ret_chain(g, tl, gam):
            srg = SR_t[:, g, :].rearrange("p (a b) -> p a b", a=2)
            chunks = tl["chunks"]
            nch = len(chunks)

            def stB(c):
                off, C = chunks[c]
                par = c % 2
                kdcol = (C_KD128 if C == 128 else C_KD16) + g
                VT, KD, PT = cr(oVT[par], oVT[par] + 256, C), cr(oKD[par], oKD[par] + 256, C), cr(oPT[par], oPT[par] + C, C)
                pv = nps()
                for e in range(2):
                    I("pe", "transpose", R=[bVF[e], CSb], W=[pv], inc=(e == 1), out=pv.t[:C, e * 128:(e + 1) * 128],
                      in_=VF[:, e, off:off + C], identity=ident)
                I("act", "activation", R=[pv], W=[bVT[par]], out=r(VT), in_=pv.t[:C, :256], func=AF.Copy)
                pk = nps()
                for e in range(2):
                    I("pe", "transpose", R=[bKR[e], CSb], W=[pk], inc=(e == 1), out=pk.t[:C, e * 128:(e + 1) * 128],
                      in_=KR[:, e, off:off + C], identity=ident)
                I("dve", "tensor_scalar", R=[pk, CSb], W=[bKD[par]], out=r(KD), in0=pk.t[:C, :256], scalar1=CS_t[:C, kdcol:kdcol + 1],
                  scalar2=None, op0=ALU.mult)
                p3 = nps()
                for e in range(2):
                    I("pe", "matmul", R=[bKR[e], bQR[e]], W=[p3], inc=(e == 1), out=p3.t[:C, :C], lhsT=r(KR[:, e, off:off + C]),
                      rhs=r(QR[:, e, off:off + C]), start=(e == 0), stop=(e == 1))
                I("dve", "tensor_tensor", R=[p3, CSb], W=[bPT[par]], out=r(PT), in0=p3.t[:C, :C],
                  in1=CS_t[:C, C_DM + g * 128:C_DM + g * 128 + C], op=ALU.mult)

            def stC(c):
                off, C = chunks[c]
                par = c % 2
                VT, PT = cr(oVT[par], oVT[par] + 256, C), cr(oPT[par], oPT[par] + C, C)
                pb = nps()
                for e in range(2):
                    I("pe", "matmul", R=[bQR[e], SRb[g]], W=[pb], inc=(e == 1), out=pb.t[:C, :256], lhsT=r(QR[:, e, off:off + C]),
                      rhs=r(srg[:, e, :]), start=(e == 0), stop=(e == 1))
                I("dve", "tensor_scalar", R=[pb, CSb], W=[bTBS[par]], out=ct(oTBS[par], oTBS[par] + 256, C), in0=pb.t[:C, :256],
                  scalar1=CS_t[:C, C_RD + g:C_RD + g + 1], scalar2=None, op0=ALU.mult)
                pa = nps()
                I("pe", "matmul", R=[bPT[par], bVT[par]], W=[pa], out=pa.t[:C, :256], lhsT=r(PT), rhs=r(VT), start=True, stop=True)
                I("dve", "tensor_tensor", R=[pa, bTBS[par]], W=[bO[par]], out=ct(oO[par], oO[par] + 256, C), in0=pa.t[:C, :256],
                  in1=ct(oTBS[par], oTBS[par] + 256, C), op=ALU.add)
                for e in range(2):
                    pS = nps()
                    I("pe", "matmul", R=[bKD[par], bVT[par]], W=[pS], out=pS.t[:, :256],
                      lhsT=r(CR_t[:C, oKD[par] + e * 128:oKD[par] + (e + 1) * 128]), rhs=r(VT), start=True, stop=True)
                    I("dve", "scalar_tensor_tensor", R=[pS, SRb[g]], W=[SRb[g]], out=r(srg[:, e, :]), in0=srg[:, e, :],
                      scalar=float(gam ** C), in1=pS.t[:, :256], op0=ALU.mult, op1=ALU.add)
                ret_norm(C, par)

            def stD(c):
                off, C = chunks[c]
                ret_gate(g, off, C, c % 2)
            for k in range(nch + 2):
                if k < nch:
                    stB(k)
                if 0 <= k - 1 < nch:
                    stC(k - 1)
                if 0 <= k - 2 < nch:
                    stD(k - 2)
                yield

        def hg_chain(hh, hs, tl):
            sh = SH_t[:, hh, :]
            QS, KH, LOGF, IF = ysl(hs["QS"]), ysl(hs["KH"]), ysl(hs["LOGF"]), ysl(hs["IF"])
            yQS, yKH, yLOGF, yIF = YS[hs["QS"]], YS[hs["KH"]], YS[hs["LOGF"]], YS[hs["IF"]]
            chunks = tl["chunks"]
            nch = len(chunks)
            ncol = chunks[-1][0] + chunks[-1][1]
            W0, W1, W2, W3 = (CR_t[:, o:o + N] for o in oW)
            Bw = CT_t[:, oBW:oBW + N]
            for (off, C) in chunks:
                I("dve", "tensor_tensor_scan", R=[yLOGF, SMb], W=[bBW], out=Bw[:, off:off + C], data0=ones1[:, :C],
                  data1=LOGF[:, off:off + C], initial=0.0, op0=ALU.mult, op1=ALU.add)
            for c, (off, C) in enumerate(chunks):
                mid = off + max(C // 2 - 1, 0)
                I("dve", "tensor_scalar", R=[bBW], W=[bAUX], out=CT_t[:, oAUX + c:oAUX + c + 1], in0=Bw[:, mid:mid + 1], scalar1=-1.0,
                  scalar2=None, op0=ALU.mult)
            for c, (off, C) in enumerate(chunks):
                mid = off + max(C // 2 - 1, 0)
                last = off + C - 1
                I("act", "activation", R=[bBW, bAUX], W=[bW[0]], out=r(W0[:, off:off + C]), in_=Bw[:, off:off + C], func=AF.Exp,
                  bias=CT_t[:, oAUX + c:oAUX + c + 1], scale=1.0)
                I("act", "activation", R=[bBW], W=[bW[1]], out=r(W1[:, off:off + C]), in_=Bw[:, off:off + C], func=AF.Exp,
                  bias=Bw[:, mid:mid + 1], scale=-1.0)
                I("act", "activation", R=[bBW], W=[bW[3]], out=r(W3[:, off:off + C]), in_=Bw[:, off:off + C], func=AF.Exp,
                  bias=Bw[:, last:last + 1], scale=-1.0)
                I("act", "activation", R=[bBW], W=[bAUX], out=CT_t[:, oAUX + 8 + c:oAUX + 9 + c], in_=Bw[:, last:last + 1], func=AF.Exp)
            I("act", "activation", R=[bBW], W=[bW[2]], out=r(W2[:, :ncol]), in_=Bw[:, :ncol], func=AF.Exp)
            I("dve", "tensor_tensor", R=[yQS, bW[0]], W=[bW[0]], out=r(W0[:, :ncol]), in0=QS[:, :ncol], in1=W0[:, :ncol], op=ALU.mult)
            I("pool", "tensor_tensor", R=[yKH, bW[1]], W=[bW[1]], out=r(W1[:, :ncol]), in0=KH[:, :ncol], in1=W1[:, :ncol], op=ALU.mult)
            I("dve", "tensor_tensor", R=[yQS, bW[2]], W=[bW[2]], out=r(W2[:, :ncol]), in0=QS[:, :ncol], in1=W2[:, :ncol], op=ALU.mult)
            I("pool", "tensor_tensor", R=[yKH, bW[3]], W=[bW[3]], out=r(W3[:, :ncol]), in0=KH[:, :ncol], in1=W3[:, :ncol], op=ALU.mult)
            for _ in range(4):
                yield

            def stB(c):
                off, C = chunks[c]
                par = c % 2
                VT, KD, PT = cr(oVT[par], oVT[par] + 128, C), cr(oKD[par], oKD[par] + 128, C), cr(oPT[par], oPT[par] + C, C)
                pi = nps()
                I("pe", "transpose", R=[yIF, CSb], W=[pi], out=pi.t[:C, :128], in_=IF[:, off:off + C], identity=ident)
                I("act", "activation", R=[pi], W=[bVT[par]], out=r(VT), in_=pi.t[:C, :128], func=AF.Copy)
                pk = nps()
                I("pe", "transpose", R=[bW[3], CSb], W=[pk], out=pk.t[:C, :128], in_=W3[:, off:off + C], identity=ident)
                I("act", "activation", R=[pk], W=[bKD[par]], out=r(KD), in_=pk.t[:C, :128], func=AF.Copy)
                p3 = nps()
                I("pe", "matmul", R=[bW[0], bW[1]], W=[p3], out=p3.t[:C, :C], lhsT=r(W1[:, off:off + C]), rhs=r(W0[:, off:off + C]),
                  start=True, stop=True)
                I("dve", "tensor_tensor", R=[p3, CSb], W=[bPT[par]], out=r(PT), in0=p3.t[:C, :C], in1=caus[:C, :C], op=ALU.mult)

            def stC(c):
                off, C = chunks[c]
                par = c % 2
                VT, KD, PT = cr(oVT[par], oVT[par] + 128, C), cr(oKD[par], oKD[par] + 128, C), cr(oPT[par], oPT[par] + C, C)
                pa = nps()
                I("pe", "matmul", R=[bPT[par], bVT[par]], W=[pa], inc=False, out=pa.t[:C, :128], lhsT=r(PT), rhs=r(VT),
                  start=True, stop=False)
                I("pe", "matmul", R=[bW[2], SHb[hh]], W=[pa], out=pa.t[:C, :128], lhsT=r(W2[:, off:off + C]), rhs=r(sh),
                  start=False, stop=True)
                pS = nps()
                I("pe", "matmul", R=[bKD[par], bVT[par]], W=[pS], out=pS.t[:, :128], lhsT=r(KD), rhs=r(VT), start=True, stop=True)
                I("dve", "scalar_tensor_tensor", R=[pS, bAUX, SHb[hh]], W=[SHb[hh]], out=r(sh), in0=sh,
                  scalar=CT_t[:, oAUX + 8 + c:oAUX + 9 + c], in1=pS.t[:, :128], op0=ALU.mult, op1=ALU.add)
                hg_norm(C, pa, par)

            def stD(c):
                off, C = chunks[c]
                hg_gate(hh, hs, off, C, c % 2)
            for k in range(nch + 2):
                if k < nch:
                    stB(k)
                if 0 <= k - 1 < nch:
                    stC(k - 1)
                if 0 <= k - 2 < nch:
                    stD(k - 2)
                yield

        XS_v = CR_t

        bVM = [Buf("cVM0"), Buf("cVM1")]

        def sample_ret(g, tl, gam):
            S.retire(SSh, SSb)
            S.retire(XSB, bVM)
            so = tl["soff"]
            Ktm, Vtm = XS_v[:16, 0:256], XS_v[:16, 256:512]
            VMs = [XS_v[:16, 512:768], XS_v[:16, 768:1024]]
            pk = nps()
            for e in range(2):
                I("pe", "transpose", R=[bKR[e], CSb], W=[pk], inc=(e == 1), out=pk.t[:16, e * 128:(e + 1) * 128],
                  in_=KR[:, e, so:so + 16], identity=ident)
            I("act", "activation", R=[pk], W=XSB, out=r(Ktm), in_=pk.t[:16, :256], func=AF.Copy)
            pv = nps()
            for e in range(2):
                I("pe", "transpose", R=[bVF[e], CSb], W=[pv], inc=(e == 1), out=pv.t[:16, e * 128:(e + 1) * 128],
                  in_=VF[:, e, so:so + 16], identity=ident)
            I("act", "activation", R=[pv], W=XSB, out=r(Vtm), in_=pv.t[:16, :256], func=AF.Copy)
            for e in range(2):
                I("dve", "tensor_tensor", R=[bQR[e], CSb], W=[QMb], out=QM_t[:, e, :].rearrange("p (a b) -> p a b", a=16),
                  in0=QR[:, e, so:so + 16].unsqueeze(1).broadcast_to([128, 16, 16]), in1=sel3, op=ALU.mult)
            po = PO
            yield
            sss = []

            def upd(s):
                ss = SSb[sst[0] % 2]
                sst[0] += 1
                sss.append(ss)
                row = (s * HR + g) * 128
                VM, bv = VMs[s % 2], bVM[s % 2]
                DMA(ss.t[:], sret_in[row:row + 128, :], ss, W=[ss])
                I("dve", "tensor_scalar", R=XSB + [CSb], W=[bv], out=r(VM), in0=Vtm, scalar1=CS_t[:16, C_OHK + s:C_OHK + s + 1],
                  scalar2=None, op0=ALU.mult)
                for e in range(2):
                    pu = nps()
                    I("pe", "matmul", R=XSB + [bv], W=[pu], out=pu.t[:, :256], lhsT=r(XS_v[:16, e * 128:(e + 1) * 128]), rhs=r(VM),
                      start=True, stop=True)
                    I("dve", "scalar_tensor_tensor", R=[pu, ss], W=[ss], out=ss.t[:, e * 256:(e + 1) * 256],
                      in0=ss.t[:, e * 256:(e + 1) * 256], scalar=float(gam), in1=pu.t[:, :256], op0=ALU.mult, op1=ALU.add)

            def out(s):
                ss = sss[s]
                row = (s * HR + g) * 128
                DMA(sret_s[row:row + 128, :], ss.t[:], ss, R=[ss], final=True)
                for e in range(2):
                    I("pe", "matmul", R=[QMb, ss], W=[po], inc=(e == 1), out=po.t[:16, :256],
                      lhsT=QM_t[:, e, s * 16:(s + 1) * 16], rhs=ss.t[:, e * 256:(e + 1) * 256],
                      start=(s == 0 and e == 0), stop=(s == NSMP - 1 and e == 1))
            upd(0)
            for s in range(NSMP):
                if s + 1 < NSMP:
                    upd(s + 1)
                out(s)
                yield
            I("act", "activation", R=[po], W=[bO[0]], out=ct(oO[0], oO[0] + 256, 16), in_=po.t[:16, :256], func=AF.Copy)
            ret_norm(16, 0)
            yield
            ret_gate(g, so, 16, 0)
            S.retire(bVM, XSB)
            yield

        def sample_hg(hh, hs, tl):
            S.retire(SSb, SSh)
            S.retire(XSB, bVM)
            so = tl["soff"]
            QS, FS, KH, IF = ysl(hs["QS"]), ysl(hs["FS"]), ysl(hs["KH"]), ysl(hs["IF"])
            Ktm, Itm = XS_v[:16, 0:128], XS_v[:16, 256:384]
            IMs = [XS_v[:16, 512:640], XS_v[:16, 768:896]]
            pk = nps()
            I("pe", "transpose", R=[YS[hs["KH"]], CSb], W=[pk], out=pk.t[:16, :128], in_=KH[:, so:so + 16], identity=ident)
            I("act", "activation", R=[pk], W=XSB, out=r(Ktm), in_=pk.t[:16, :128], func=AF.Copy)
            pv = nps()
            I("pe", "transpose", R=[YS[hs["IF"]], CSb], W=[pv], out=pv.t[:16, :128], in_=IF[:, so:so + 16], identity=ident)
            I("act", "activation", R=[pv], W=XSB, out=r(Itm), in_=pv.t[:16, :128], func=AF.Copy)
            I("dve", "tensor_tensor", R=[YS[hs["QS"]], CSb], W=[QMb], out=QM_t[:, 0, :].rearrange("p (a b) -> p a b", a=16),
              in0=QS[:, so:so + 16].unsqueeze(1).broadcast_to([128, 16, 16]), in1=sel3, op=ALU.mult)
            po = PO
            yield
            sss = []

            def upd(s):
                ss = SSh[sst[0] % 8]
                sst[0] += 1
                sss.append(ss)
                row = (s * HH + hh) * 128
                IM, bv = IMs[s % 2], bVM[s % 2]
                DMA(ss.t[:, :128], shg_in[row:row + 128, :], ss, W=[ss])
                I("dve", "tensor_scalar", R=XSB + [CSb], W=[bv], out=r(IM), in0=Itm, scalar1=CS_t[:16, C_OH + s:C_OH + s + 1],
                  scalar2=None, op0=ALU.mult)
                pu = nps()
                I("pe", "matmul", R=XSB + [bv], W=[pu], out=pu.t[:, :128], lhsT=r(Ktm), rhs=r(IM), start=True, stop=True)
                I("dve", "scalar_tensor_tensor", R=[pu, ss, YS[hs["FS"]]], W=[ss], out=ss.t[:, :128], in0=ss.t[:, :128],
                  scalar=FS[:, so + s:so + s + 1], in1=pu.t[:, :128], op0=ALU.mult, op1=ALU.add)

            def out(s):
                ss = sss[s]
                row = (s * HH + hh) * 128
                DMA(shg_s[row:row + 128, :], ss.t[:, :128], ss, R=[ss], final=True)
                I("pe", "matmul", R=[QMb, ss], W=[po], out=po.t[:16, :128], lhsT=QM_t[:, 0, s * 16:(s + 1) * 16],
                  rhs=ss.t[:, :128], start=(s == 0), stop=(s == NSMP - 1))
            upd(0)
            upd(1)
            for s in range(NSMP):
                if s + 2 < NSMP:
                    upd(s + 2)
                out(s)
                if s % 2 == 1:
                    yield
            hg_norm(16, po, 0)
            yield
            hg_gate(hh, hs, so, 16, 0)
            S.retire(bVM, XSB)
            yield

        def evac_y(k, e_off, ybufs, func=AF.Copy):
            def f(ps, hf, c0, nh):
                I("act", "activation", R=[ps], W=ybufs, out=r(YA_t[:, k * N + c0:k * N + c0 + nh]), in_=ps.t[:, :nh], func=func)
            return f

        def advance(gen, k=1):
            if gen is None:
                return None
            for _ in range(k):
                try:
                    next(gen)
                except StopIteration:
                    return None
            return gen

        def drain(gen):
            while gen is not None:
                gen = advance(gen, 1)

        def hg_proj(hh, hs, n, base, bg, k=1):
            box = [bg]

            def step():
                box[0] = advance(box[0], k)
            yq, yf, yl, yk, yi, yg, yt = (YS[hs[k]] for k in ("QS", "FS", "LOGF", "KH", "IF", "GH", "TB"))
            QS, FS, LOGF, KH, IF, GH, TB = (ysl(hs[k]) for k in ("QS", "FS", "LOGF", "KH", "IF", "GH", "TB"))
            pc[0] = base
            gemm_fm(next_piece(), n, evac_y(hs["QS"], 0, [yq], AF.Silu), bg=step)
            gemm_fm(next_piece(), n, evac_y(hs["FS"], 0, [yf], AF.Sigmoid), bg=step)
            I("dve", "tensor_scalar", R=[yf, SMb], W=[yf], out=r(FS[:, :n]), in0=FS[:, :n], scalar1=OMLc[:, hh:hh + 1],
              scalar2=LBc[:, hh:hh + 1], op0=ALU.mult, op1=ALU.add)
            I("act", "activation", R=[yf], W=[yl], out=r(LOGF[:, :n]), in_=FS[:, :n], func=AF.Ln)
            I("pool", "tensor_scalar", R=[yf], W=[yk], out=r(KH[:, :n]), in0=FS[:, :n], scalar1=-1.0, scalar2=1.0,
              op0=ALU.mult, op1=ALU.add)
            gemm_fm(next_piece(), n, evac_y(hs["IF"], 0, [yi], AF.Copy), bg=step)
            gemm_fm(next_piece(), n, evac_y(hs["GH"], 0, [yg], AF.Silu), bg=step)
            gemm_fm(next_piece(), n, evac_y(hs["TB"], 0, [yt], AF.Sigmoid), bg=step)
            I("dve", "scalar_tensor_tensor", R=[yt, yg, VCb], W=[yg], out=r(GH[:, :n]), in0=TB[:, :n],
              scalar=VC_t[:, V_HG + hh:V_HG + hh + 1], in1=GH[:, :n], op0=ALU.mult, op1=ALU.mult)
            return box[0]

        def prefix_state(ti, tl, cbase):
            n = tl["n"]
            S.retire([YA], YS)
            S.retire([CTb], CXALL)
            DMA(RT_t[:].rearrange("p a b -> p (a b)"), rot[ti], RTb, W=[RTb])
            nch = len(tl["chunks"])
            hs = HSET[0]
            oBB, oEE = oBW, N
            bBB, bEE = bBW, bO[0]
            bEEl = [bO[0], bO[1], bON[0], bON[1]]
            oRS = oSTAT[0] + 9
            ids_all = []
            for g in range(HR):
                b_ = 129 + g * 20
                ids_all += [b_ + 2, b_ + 3, b_ + 4, b_ + 5, b_ + 11, b_ + 12, b_ + 16, b_ + 17]
            ids_all.append(0)
            ipos = [0]

            def take():
                k = ipos[0]
                ipos[0] += 1
                pc[0] = ids_all[k]
                return next_piece(nxt=ids_all[k + 1])
            for g in range(HR):
                base = 129 + g * 20
                srg = SR_t[:, g, :].rearrange("p (a b) -> p a b", a=2)
                for e in range(2):
                    gemm_fm(take(), n, evac_y(e, 0, [bRAW[e]]))
                rotary(KR, bKR, n)
                for e in range(2):
                    gemm_fm(take(), n, evac_y(8 + e, 0, [bVF[e]]))
                acc = [PO2, PO]

                def rT(ci):
                    off, C = tl["chunks"][ci]
                    par = ci % 2
                    VT, KD = cr(oVT[par], oVT[par] + 256, C), cr(oKD[par], oKD[par] + 256, C)
                    dcol = C_PDEC + (cbase + ci) * 8 + g
                    pv = nps()
                    for e in range(2):
                        I("pe", "transpose", R=[bVF[e], CSb], W=[pv], inc=(e == 1), out=pv.t[:C, e * 128:(e + 1) * 128],
                          in_=VF[:, e, off:off + C], identity=ident)
                    I("act", "activation", R=[pv], W=[bVT[par]], out=r(VT), in_=pv.t[:C, :256], func=AF.Copy)
                    pk = nps()
                    for e in range(2):
                        I("pe", "transpose", R=[bKR[e], CSb], W=[pk], inc=(e == 1), out=pk.t[:C, e * 128:(e + 1) * 128],
                          in_=KR[:, e, off:off + C], identity=ident)
                    I("dve", "tensor_scalar", R=[pk, CSb], W=[bKD[par]], out=r(KD), in0=pk.t[:C, :256], scalar1=CS_t[:C, dcol:dcol + 1],
                      scalar2=None, op0=ALU.mult)

                def rM(ci):
                    off, C = tl["chunks"][ci]
                    par = ci % 2
                    VT = cr(oVT[par], oVT[par] + 256, C)
                    for e in range(2):
                        I("pe", "matmul", R=[bKD[par], bVT[par]], W=[acc[e]], out=acc[e].t[:, :256],
                          lhsT=r(CR_t[:C, oKD[par] + e * 128:oKD[par] + (e + 1) * 128]), rhs=r(VT), start=(ci == 0), stop=(ci == nch - 1))
                rT(0)
                for ci in range(nch):
                    if ci + 1 < nch:
                        rT(ci + 1)
                    rM(ci)
                for e in range(2):
                    I("dve", "tensor_tensor", R=[acc[e], SRb[g]], W=[SRb[g]], out=r(srg[:, e, :]), in0=acc[e].t[:, :256],
                      in1=srg[:, e, :], op=ALU.add)
                for hh in (2 * g, 2 * g + 1):
                    hb = base + 10 + (hh - 2 * g) * 5
                    sh = SH_t[:, hh, :]
                    yf, yl, yk, yi, yt = (YS[hs[k]] for k in ("FS", "LOGF", "KH", "IF", "TB"))
                    FS, LOGF, KH, IF, KDF = (ysl(hs[k]) for k in ("FS", "LOGF", "KH", "IF", "TB"))
                    gemm_fm(take(), n, evac_y(hs["FS"], 0, [yf], AF.Sigmoid))
                    I("dve", "tensor_scalar", R=[yf, SMb], W=[yf], out=r(FS[:, :n]), in0=FS[:, :n], scalar1=OMLc[:, hh:hh + 1],
                      scalar2=LBc[:, hh:hh + 1], op0=ALU.mult, op1=ALU.add)
                    I("act", "activation", R=[yf], W=[yl], out=r(LOGF[:, :n]), in_=FS[:, :n], func=AF.Ln)
                    I("pool", "tensor_scalar", R=[yf], W=[yk], out=r(KH[:, :n]), in0=FS[:, :n], scalar1=-1.0, scalar2=1.0,
                      op0=ALU.mult, op1=ALU.add)
                    gemm_fm(take(), n, evac_y(hs["IF"], 0, [yi], AF.Copy))
                    I("dve", "tensor_tensor_scan", R=[yl, SMb], W=[bBB], out=CT_t[:, oBB:oBB + n],
                      data0=ones1[:, 0:1].broadcast_to([128, n]), data1=LOGF[:, :n], initial=0.0, op0=ALU.mult, op1=ALU.add)
                    I("act", "activation", R=[bBB], W=bEEl, out=CT_t[:, oEE:oEE + n], in_=CT_t[:, oBB:oBB + n], func=AF.Exp,
                      bias=CT_t[:, oBB + n - 1:oBB + n], scale=-1.0)
                    I("act", "activation", R=[bBB], W=[bSTT[0]], out=ct(oRS, oRS + 1), in_=CT_t[:, oBB + n - 1:oBB + n], func=AF.Exp)
                    I("dve", "tensor_tensor", R=[yk] + bEEl, W=[yt], out=r(KDF[:, :n]), in0=KH[:, :n], in1=CT_t[:, oEE:oEE + n],
                      op=ALU.mult)
                    def hT(ci, yi=yi, yt=yt, IF=IF, KDF=KDF):
                        off, C = tl["chunks"][ci]
                        par = ci % 2
                        VT, KD = cr(oVT[par], oVT[par] + 128, C), cr(oKD[par], oKD[par] + 128, C)
                        pi = nps()
                        I("pe", "transpose", R=[yi, CSb], W=[pi], out=pi.t[:C, :128], in_=IF[:, off:off + C], identity=ident)
                        I("act", "activation", R=[pi], W=[bVT[par]], out=r(VT), in_=pi.t[:C, :128], func=AF.Copy)
                        pk = nps()
                        I("pe", "transpose", R=[yt, CSb], W=[pk], out=pk.t[:C, :128], in_=KDF[:, off:off + C], identity=ident)
                        I("dve", "tensor_copy", R=[pk], W=[bKD[par]], out=r(KD), in_=pk.t[:C, :128])

                    def hM(ci):
                        off, C = tl["chunks"][ci]
                        par = ci % 2
                        VT, KD = cr(oVT[par], oVT[par] + 128, C), cr(oKD[par], oKD[par] + 128, C)
                        I("pe", "matmul", R=[bKD[par], bVT[par]], W=[PO], out=PO.t[:, :128], lhsT=r(KD), rhs=r(VT), start=(ci == 0),
                          stop=(ci == nch - 1))
                    hT(0)
                    for ci in range(nch):
                        if ci + 1 < nch:
                            hT(ci + 1)
                        hM(ci)
                    I("dve", "scalar_tensor_tensor", R=[PO, bSTT[0], SHb[hh]], W=[SHb[hh]], out=r(sh), in0=sh, scalar=ct(oRS, oRS + 1),
                      in1=PO.t[:, :128], op0=ALU.mult, op1=ALU.add)
            S.retire(YS, [YA])
            S.retire(CXALL, [CTb])

        sst = [0]

        def mixer(ti, tl):
            n = tl["n"]
            conv_split[0] = 1408
            YTb = [Buf("YT%d" % k) for k in range(KC)]
            M["YTb"] = YTb
            S.retire([YA], YS)
            S.retire([CTb], CXALL)
            S.retire(HG, YTb)
            DMA(RT_t[:].rearrange("p a b -> p (a b)"), rot[ti], RTb, W=[RTb])
            last = tl is otiles[-1]
            kk = 3 if tl["smp"] else 1

            def seqg(*gens):
                for g_ in gens:
                    if g_ is not None:
                        yield from g_
            bgB = None
            prevB = None
            for g in range(HR):
                gam = 1.0 - 2.0 ** (-5.0 - g)
                base = 129 + g * 20
                box = [bgB]

                def step():
                    box[0] = advance(box[0], kk)
                pc[0] = base
                for e in range(2):
                    gemm_fm(next_piece(), n, evac_y(e, 0, [bRAW[e]]), bg=step)
                rotary(QR, bQR, n)
                for e in range(2):
                    gemm_fm(next_piece(), n, evac_y(e, 0, [bRAW[e]]), bg=step)
                drain(box[0])
                if prevB is not None and last:
                    DMA(shg_p[prevB * 128:(prevB + 1) * 128, :], SH_t[:, prevB, :], SHb[prevB], R=[SHb[prevB]], final=True)
                rotary(KR, bKR, n)
                for e in range(2):
                    gemm_fm(next_piece(), n, evac_y(8 + e, 0, [bVF[e]]))
                for e in range(2):
                    gemm_fm(next_piece(), n, evac_y(10 + e, 0, [bGR[e]], AF.Silu))
                for e in range(2):
                    gemm_fm(next_piece(), n, evac_y(12, 0, bTA, AF.Sigmoid))
                    I("dve", "tensor_tensor", R=[bGR[e]] + bTA, W=[bGR[e]], out=r(GR[:, e, :n]), in0=GR[:, e, :n], in1=TA[:, :n],
                      op=ALU.mult)
                hA, hB = 2 * g, 2 * g + 1
                rem = hg_proj(hA, HSET[0], n, base + 10, seqg(ret_chain(g, tl, gam), sample_ret(g, tl, gam) if tl["smp"] else None), kk)
                drain(rem)
                if last:
                    DMA(sret_p[g * 128:(g + 1) * 128, :], SR_t[:, g, :], SRb[g], R=[SRb[g]], final=True)
                rem = hg_proj(hB, HSET[1], n, base + 15, seqg(hg_chain(hA, HSET[0], tl), sample_hg(hA, HSET[0], tl) if tl["smp"] else None), kk)
                drain(rem)
                if last:
                    DMA(shg_p[hA * 128:(hA + 1) * 128, :], SH_t[:, hA, :], SHb[hA], R=[SHb[hA]], final=True)
                bgB = seqg(hg_chain(hB, HSET[1], tl), sample_hg(hB, HSET[1], tl) if tl["smp"] else None)
                prevB = hB
            drain(bgB)
            if last:
                DMA(shg_p[prevB * 128:(prevB + 1) * 128, :], SH_t[:, prevB, :], SHb[prevB], R=[SHb[prevB]], final=True)
            S.retire(YS, [YA])
            S.retire(CXALL, [CTb])
            pc[0] = 129 + 160
            for m in range(KC):
                def ev_out(ps, hf, c0, nh, m=m):
                    I("dve", "scalar_tensor_tensor", R=[ps, XA], W=[YA], out=r(YAv[:, m, c0:c0 + nh]), in0=XA_t[:, m, c0:c0 + nh],
                      scalar=ALPHA, in1=ps.t[:, :nh], op0=ALU.mult, op1=ALU.add)
                gemm_fm(next_piece(), n, ev_out, xbufs=YTb, xt=YT)
            S.retire(YTb, HG)
            conv_split[0] = 1024
            layer_norm(n, 1)

        cb = 0
        for ti, tl in enumerate(ptiles):
            pc[0] = 0
            load_x(tl)
            ffn_ln(tl["n"], 0, after=131)
            prefix_state(ti, tl, cb)
            cb += len(tl["chunks"])
        for g in range(HR):
            I("dve", "tensor_scalar", R=[SRb[g], SELb], W=[SRb[g]], out=r(SR_t[:, g, :]), in0=SR_t[:, g, :], scalar1=SEL_t[:, 0:1],
              scalar2=None, op0=ALU.mult)
        for g in range(HH):
            I("pool", "tensor_scalar", R=[SHb[g], SELb], W=[SHb[g]], out=r(SH_t[:, g, :]), in0=SH_t[:, g, :], scalar1=SEL_t[:, 0:1],
              scalar2=None, op0=ALU.mult)
        for ti, tl in enumerate(otiles):
            pc[0] = 0
            load_x(tl)
            if stop_stage >= 1:
                ffn_ln(tl["n"], 0)
            pc[0] = 129
            if stop_stage >= 2:
                mixer(len(ptiles) + ti, tl)
            pc[0] = 129 + 176
            if stop_stage >= 3:
                ffn_ln(tl["n"], 2, after=(0 if ti + 1 < len(otiles) else None))
            store_y(tl)
        S.finish()

        with nc.Block() as block:
            @block.sync
            def _(h):
                S.replay("sp", h)

            @block.tensor
            def _(h):
                S.replay("pe", h)

            @block.scalar
            def _(h):
                S.replay("act", h)

            @block.vector
            def _(h):
                S.replay("dve", h)

            @block.gpsimd
            def _(h):
                S.replay("pool", h)
    return nc, (ptiles, otiles), NTOT, NMAX


def _fm_piece(W, cols):
    return np.ascontiguousarray(W[:, cols].reshape(KC, 128, 128).transpose(1, 0, 2)).reshape(128, 2048)


def _down_piece(Wd, k0, g, c0, ncols):
    blk = Wd[k0 * 128:(k0 + g) * 128, c0:c0 + ncols]
    return np.ascontiguousarray(blk.reshape(g, 128, ncols).transpose(1, 0, 2)).reshape(128, 2048)


def _ffn_pieces(out, p, wg, wu, wd):
    k0 = 0
    for g in GROUPS:
        for jj in range(g):
            cols = np.arange((k0 + jj) * 128, (k0 + jj + 1) * 128)
            out[p] = _fm_piece(wg, cols); p += 1
            out[p] = _fm_piece(wu, cols); p += 1
        ncols = 2048 // g
        for dp in range(g):
            out[p] = _down_piece(wd, k0, g, dp * ncols, ncols); p += 1
        k0 += g
    return p


def make_wall(i):
    wall = np.empty((NPIECE, 128, 2048), np.float32)
    p = _ffn_pieces(wall, 0, i["ffn1_w_gate"][0], i["ffn1_w_up"][0], i["ffn1_w_down"][0])
    win = i["w_in"][0]
    a128 = np.arange(128)
    for g in range(HR):
        base = g * 256
        for proj in (0, 1):
            for e in range(2):
                wall[p] = _fm_piece(win, proj * D + base + 2 * a128 + e); p += 1
        for proj in (2, 3, 8):
            for e in range(2):
                wall[p] = _fm_piece(win, proj * D + base + e * 128 + a128); p += 1
        for hh in (2 * g, 2 * g + 1):
            for proj in (4, 5, 6, 7, 9):
                wall[p] = _fm_piece(win, proj * D + hh * 128 + a128); p += 1
    wo = i["w_out"][0]
    for m in range(KC):
        wall[p] = _fm_piece(wo, m * 128 + a128); p += 1
    p = _ffn_pieces(wall, p, i["ffn2_w_gate"][0], i["ffn2_w_up"][0], i["ffn2_w_down"][0])
    assert p == NPIECE
    return wall


def make_consts():
    c = np.zeros((128, C_END), np.float64)
    c[:, C_ID:C_ID + 128] = np.eye(128, dtype=np.float32)
    j = np.arange(128)[:, None].astype(np.float64)
    i = np.arange(128)[None, :].astype(np.float64)
    c[:, C_CAUS:C_CAUS + 128] = (i >= j)
    for h in range(HR):
        gam = 1.0 - 2.0 ** (-5.0 - h)
        c[:, C_DM + h * 128:C_DM + (h + 1) * 128] = np.where(i >= j, gam ** np.maximum(i - j, 0), 0.0) / 16.0
        c[:, C_KD128 + h] = gam ** (127.0 - np.arange(128)) / 16.0
        c[:PRE, C_KD16 + h] = gam ** (PRE - 1.0 - np.arange(PRE)) / 16.0
        c[:, C_RD + h] = gam ** (np.arange(128) + 1.0)
    sel = np.zeros((16, 16), np.float32)
    sel[np.arange(16), np.arange(16)] = 1.0
    c[:, C_SEL:C_SEL + 256] = sel.reshape(1, 256)
    c[:16, C_OH:C_OH + 16] = np.eye(16, dtype=np.float32)
    c[:16, C_OHK:C_OHK + 16] = np.eye(16, dtype=np.float32) / 16.0
    return c


def add_prefix_decay(c, ptiles):
    total = ptiles[-1]["g0"] + ptiles[-1]["n"]
    slot = 0
    for tl in ptiles:
        for (off, C) in tl["chunks"]:
            gpos = tl["g0"] + off + np.arange(C)
            for h in range(HR):
                gam = 1.0 - 2.0 ** (-5.0 - h)
                c[:C, C_PDEC + slot * 8 + h] = gam ** (total - 1.0 - gpos) / 16.0
            slot += 1
    assert slot <= 16
    return c


def make_rot(ptiles, otiles, part, NMAX):
    tiles = ptiles + otiles
    nt = len(tiles)
    half = ptiles[-1]["g0"] + ptiles[-1]["n"]
    rot = np.zeros((nt, 128, 2, NMAX), np.float32)
    inv = (10000.0 ** (-np.arange(0, 256, 2, dtype=np.float32) / np.float32(256))).astype(np.float32)
    for ti, tl in enumerate(tiles):
        base = tl["g0"] + (part * half if ti >= len(ptiles) else 0)
        pos = (base + np.arange(tl["n"])).astype(np.float32)
        if tl["smp"]:
            pos[tl["soff"]:tl["soff"] + 16] = PAST
        ang = (pos[None, :] * inv[:, None]).astype(np.float32)
        rot[ti, :, 0, :tl["n"]] = np.cos(ang).astype(np.float32)
        rot[ti, :, 1, :tl["n"]] = np.sin(ang).astype(np.float32)
    return rot.reshape(nt, 128, 2 * NMAX)


def colvec(v):
    return np.ascontiguousarray(np.asarray(v, np.float32).reshape(KC, 128).T)


_CACHE = {}


def kernel(**inputs):
    i = {k: np.asarray(v) for k, v in inputs.items()}
    B, SEQ, _ = i["x_prompt"].shape
    nmain = (SEQ + NMETA) // 2 // 128
    stop_stage = int(_CACHE.get("stop_stage", 99))
    key = (nmain, stop_stage)
    if key not in _CACHE:
        _CACHE[key] = build(nmain, stop_stage)
    nc, (ptiles, otiles), NTOT, NMAX = _CACHE[key]
    half = ptiles[-1]["g0"] + ptiles[-1]["n"]
    assert 2 * half == SEQ + NMETA
    wall = make_wall(i)
    cst = add_prefix_decay(make_consts(), ptiles).astype(np.float32)
    rots = [make_rot(ptiles, otiles, part, NMAX) for part in range(2)]
    vecs = np.zeros((128, V_END), np.float32)
    for li, (gk, bk) in enumerate((("ln1_g", "ln1_b"), ("ln2_g", "ln2_b"), ("ln3_g", "ln3_b"))):
        vecs[:, V_LN + li * 32:V_LN + li * 32 + 16] = colvec(i[gk][0])
        vecs[:, V_LN + li * 32 + 16:V_LN + li * 32 + 32] = colvec(i[bk][0])
    vecs[:, V_LB0:V_LB0 + 16] = colvec(i["hgrn_lb_logits"][0])
    vecs[:, V_LB1:V_LB1 + 16] = colvec(i["hgrn_lb_logits"][1])
    vecs[:, V_HG:V_HG + 16] = colvec(i["hgrn_norm_g"][0])
    in_maps = []
    for c in range(8):
        b, part = c // 2, c % 2
        full = np.concatenate([i["meta_tokens"], i["x_prompt"][b]], axis=0)
        prefix = np.zeros((half, D), np.float32) if part == 0 else full[:half]
        xin = np.concatenate([prefix, full[part * half:(part + 1) * half], i["x_sample"][16 * c:16 * c + 16, 0, :]], axis=0)
        in_maps.append(dict(
            xin=np.ascontiguousarray(xin, np.float32), wall=wall, rot=rots[part], vecs=vecs, cst=cst,
            selv=np.full((128, 1), float(part), np.float32),
            sret_in=np.ascontiguousarray(i["state_ret"][0, 16 * c:16 * c + 16]).reshape(NSMP * HR * 128, 512),
            shg_in=np.ascontiguousarray(i["state_hgrn"][0, 16 * c:16 * c + 16]).reshape(NSMP * HH * 128, 128)))
    res = run_bass_kernel_spmd(nc, in_maps, core_ids=list(range(8)))
    R = res.results
    y_prompt = np.stack([np.concatenate([R[2 * b]["y"][:half], R[2 * b + 1]["y"][:half]], axis=0)[NMETA:] for b in range(B)], axis=0)
    y_sample = np.concatenate([R[c]["y"][half:half + 16] for c in range(8)], axis=0)[:, None, :]
    srp = np.stack([R[2 * b + 1]["sret_p"].reshape(HR, 256, 256) for b in range(B)], axis=0)[None]
    srs = np.concatenate([R[c]["sret_s"].reshape(NSMP, HR, 256, 256) for c in range(8)], axis=0)[None]
    shp = np.stack([R[2 * b + 1]["shg_p"].reshape(HH, 128, 128) for b in range(B)], axis=0)[None]
    shs = np.concatenate([R[c]["shg_s"].reshape(NSMP, HH, 128, 128) for c in range(8)], axis=0)[None]
    return (y_prompt.astype(np.float32), y_sample.astype(np.float32), srp.astype(np.float32),
            srs.astype(np.float32), shp.astype(np.float32), shs.astype(np.float32))
```
